# Optimizing a Trainium2 kernel written in Bass

```python
import jax
import jax.numpy as jnp
from jax import lax
import numpy as np

D_MODEL = 1024
BATCH = 8
SEQ = 4096
DEPTH = 2

GRID_W = 64
CTX_LEN = 256
D_MIX = D_MODEL
CONV_DIM = D_MIX // 4
CONV_WIDTH = 3
GMLP_DIM = D_MIX // 4
GMLP_HEADS = 4
GMLP_HEAD_DIM = GMLP_DIM // GMLP_HEADS
CHUNK = 128
RET_DIM = D_MIX // 2
RET_HEADS = 8
RET_HEAD_DIM = RET_DIM // RET_HEADS
RET_SCALE = RET_HEAD_DIM ** -0.5
D_FF = 4 * D_MODEL
N_MOD = 6
LN_EPS = 1e-5
DEEPNORM_ALPHA = (2 * DEPTH) ** 0.25
DEEPNORM_BETA = (8 * DEPTH) ** -0.25

A_X_END = CONV_DIM
A_B_END = 2 * CONV_DIM
A_C_END = 3 * CONV_DIM
B_U_END = A_C_END + GMLP_DIM
B_V_END = B_U_END + GMLP_DIM
Q_END = B_V_END + RET_DIM
K_END = Q_END + RET_DIM
V_END = K_END + RET_DIM
D_IN = V_END + RET_DIM
SPLIT_POINTS = (A_X_END, A_B_END, A_C_END, B_U_END, B_V_END, Q_END, K_END, V_END)

kernel_name = 'hybrid_conv_gmlp_retention_dit'


def layer_norm(x, g, b):
    xf = x.astype(jnp.float32)
    mu = jnp.mean(xf, axis=-1, keepdims=True)
    var = jnp.mean(jnp.square(xf - mu), axis=-1, keepdims=True)
    y = (xf - mu) * lax.rsqrt(var + LN_EPS) * g.astype(jnp.float32) + b.astype(jnp.float32)
    return y.astype(x.dtype)


def ada_modulation(cond, w_ada, b_ada):
    m = jax.nn.silu(cond) @ w_ada + b_ada
    return jnp.split(m, N_MOD, axis=-1)


def modulate(x, shift, scale):
    return x * (1 + scale) + shift


def centred_conv3(z, w):
    n = z.shape[-2]
    zp = jnp.pad(z, [(0, 0)] * (z.ndim - 2) + [(1, 1), (0, 0)])
    return w[0] * zp[..., 0:n, :] + w[1] * zp[..., 1:n + 1, :] + w[2] * zp[..., 2:n + 2, :]


def conv_mixer(xin, gate_b, gate_c, w_conv, rows):
    z = gate_c * xin
    if rows is None:
        z = centred_conv3(z, w_conv)
    else:
        bsz, n, ch = z.shape
        z = centred_conv3(z.reshape(bsz, rows, GRID_W, ch), w_conv).reshape(bsz, n, ch)
    return gate_b * z


def chunk_mlp(u, v, ln_g, ln_b, w_s, b_s):
    u = jax.nn.gelu(u)
    v = layer_norm(jax.nn.gelu(v), ln_g, ln_b)
    bsz, n, _ = v.shape
    vc = v.reshape(bsz, n // CHUNK, CHUNK, GMLP_HEADS, GMLP_HEAD_DIM)
    mixed = jnp.einsum('hpq,bcqhd->bcphd', w_s, vc) + b_s.T[None, None, :, :, None]
    return u * mixed.reshape(bsz, n, GMLP_DIM)


def ret_heads(t):
    bsz, n, _ = t.shape
    return t.reshape(bsz, n, RET_HEADS, RET_HEAD_DIM).astype(jnp.float32)


def retention_chunked(q, k, v, log_gamma, s0, include_diag):
    bsz, n, h, d = q.shape
    nc = n // CHUNK
    pos = jnp.arange(CHUNK, dtype=jnp.float32)
    rel = pos[:, None] - pos[None, :]
    mask = (rel >= 0) if include_diag else (rel > 0)
    lg = log_gamma[:, None, None]
    decay_in = jnp.where(mask, jnp.exp(lg * jnp.where(mask, rel, 0.0)), 0.0)
    decay_q = jnp.exp(log_gamma[:, None] * (pos + 1.0))[None, :, :, None]
    decay_k = jnp.exp(log_gamma[:, None] * (CHUNK - 1.0 - pos))[None, :, :, None]
    decay_chunk = jnp.exp(log_gamma * CHUNK)[None, :, None, None]

    def to_chunks(t):
        return t.reshape(bsz, nc, CHUNK, h, d).transpose(1, 0, 3, 2, 4)

    def step(s, qkv):
        qc, kc, vc = qkv
        scores = jnp.einsum('bhid,bhjd->bhij', qc, kc) * decay_in
        o = jnp.einsum('bhij,bhje->bhie', scores, vc) + jnp.einsum('bhid,bhde->bhie', qc * decay_q, s)
        s = s * decay_chunk + jnp.einsum('bhjd,bhje->bhde', kc * decay_k, vc)
        return s, o

    _, o = lax.scan(step, s0, (to_chunks(q), to_chunks(k), to_chunks(v)))
    return o.transpose(1, 0, 3, 2, 4).reshape(bsz, n, h, d)


def context_states(k, v, lg_f, lg_b):
    n = k.shape[1]
    pos = jnp.arange(n, dtype=jnp.float32)
    w_f = jnp.exp(lg_f[None, :] * (n - 1.0 - pos)[:, None])
    w_b = jnp.exp(lg_b[None, :] * pos[:, None])
    s_f = jnp.einsum('blhd,blhe->bhde', k * w_f[None, :, :, None], v)
    s_b = jnp.einsum('blhd,blhe->bhde', k * w_b[None, :, :, None], v)
    return s_f, s_b


def retention_mixer(q, k, v, g, lg_f, lg_b, s_f, s_b):
    bsz, n, _ = q.shape
    q, k, v = ret_heads(q) * RET_SCALE, ret_heads(k), ret_heads(v)
    o_f = retention_chunked(q, k, v, lg_f, s_f, True)
    o_b = jnp.flip(retention_chunked(jnp.flip(q, 1), jnp.flip(k, 1), jnp.flip(v, 1), lg_b, s_b, False), 1)
    o = o_f + o_b
    mu = jnp.mean(o, axis=-1, keepdims=True)
    var = jnp.mean(jnp.square(o - mu), axis=-1, keepdims=True)
    o = (o - mu) * lax.rsqrt(var + LN_EPS)
    return jax.nn.silu(g) * o.reshape(bsz, n, RET_DIM).astype(g.dtype)


def token_mixers(proj, w_conv, gm_ln_g, gm_ln_b, gm_ws, gm_bs, lg_f, lg_b, s_f, s_b, rows):
    xin, gate_b, gate_c, u, v, q, k, vr, g = jnp.split(proj, SPLIT_POINTS, axis=-1)
    y_conv = conv_mixer(xin, gate_b, gate_c, w_conv, rows)
    y_gmlp = chunk_mlp(u, v, gm_ln_g, gm_ln_b, gm_ws, gm_bs)
    y_ret = retention_mixer(q, k, vr, g, lg_f, lg_b, s_f, s_b)
    return jnp.concatenate([y_conv, y_gmlp, y_ret], axis=-1)


def sq_relu_mlp(h, w1, w2):
    return jnp.square(jax.nn.relu(h @ w1)) @ w2


def setup_inputs(seed: int = 0) -> dict:
    key = jax.random.key(seed)
    ks = jax.random.split(key, 19)

    def nrm(k, shape, s):
        return jax.random.normal(k, shape, jnp.float32) * s

    base_decay = jnp.log(-jnp.log1p(-(2.0 ** (-5.0 - jnp.arange(RET_HEADS, dtype=jnp.float32)))))
    return {
        'x': nrm(ks[0], (BATCH, SEQ, D_MODEL), 1.0),
        'c': nrm(ks[1], (BATCH, D_MODEL), 1.0),
        'ctx': nrm(ks[2], (BATCH, CTX_LEN, D_MODEL), 1.0),
        'c_ctx': nrm(ks[3], (D_MODEL,), 1.0),
        'w_ada': nrm(ks[4], (DEPTH, D_MODEL, N_MOD * D_MODEL), 0.5 * D_MODEL ** -0.5),
        'b_ada': nrm(ks[5], (DEPTH, N_MOD * D_MODEL), 0.02),
        'w_in': nrm(ks[6], (DEPTH, D_MODEL, D_IN), D_MODEL ** -0.5),
        'conv_w': nrm(ks[7], (DEPTH, CONV_WIDTH, CONV_DIM), CONV_WIDTH ** -0.5),
        'gmlp_ln_g': 1.0 + nrm(ks[8], (DEPTH, GMLP_DIM), 0.02),
        'gmlp_ln_b': nrm(ks[9], (DEPTH, GMLP_DIM), 0.02),
        'gmlp_ws': nrm(ks[10], (DEPTH, GMLP_HEADS, CHUNK, CHUNK), CHUNK ** -0.5),
        'gmlp_bs': 1.0 + nrm(ks[11], (DEPTH, GMLP_HEADS, CHUNK), 0.02),
        'ret_decay_fwd': base_decay + nrm(ks[12], (DEPTH, RET_HEADS), 0.01),
        'ret_decay_bwd': base_decay + nrm(ks[13], (DEPTH, RET_HEADS), 0.01),
        'w_out': nrm(ks[14], (DEPTH, D_MIX, D_MODEL), DEEPNORM_BETA * D_MIX ** -0.5),
        'w_ff1': nrm(ks[15], (DEPTH, D_MODEL, D_FF), D_MODEL ** -0.5),
        'w_ff2': nrm(ks[16], (DEPTH, D_FF, D_MODEL), DEEPNORM_BETA * D_FF ** -0.5),
        'ln_g': 1.0 + nrm(ks[17], (DEPTH, 2, D_MODEL), 0.02),
        'ln_b': nrm(ks[18], (DEPTH, 2, D_MODEL), 0.02),
    }


def reference(x, c, ctx, c_ctx, w_ada, b_ada, w_in, conv_w, gmlp_ln_g, gmlp_ln_b, gmlp_ws, gmlp_bs,
              ret_decay_fwd, ret_decay_bwd, w_out, w_ff1, w_ff2, ln_g, ln_b):
    bsz, n_lat, _ = x.shape
    rows = n_lat // GRID_W
    xc = ctx
    zero_state = jnp.zeros((bsz, RET_HEADS, RET_HEAD_DIM, RET_HEAD_DIM), jnp.float32)
    for l in range(DEPTH):
        last = l == DEPTH - 1
        sh_m, sc_m, gt_m, sh_f, sc_f, gt_f = [m[:, None, :] for m in ada_modulation(c, w_ada[l], b_ada[l])]
        csh_m, csc_m, cgt_m, csh_f, csc_f, cgt_f = ada_modulation(c_ctx, w_ada[l], b_ada[l])
        lg_f = -jnp.exp(ret_decay_fwd[l].astype(jnp.float32))
        lg_b = -jnp.exp(ret_decay_bwd[l].astype(jnp.float32))

        hc = modulate(xc, csh_m, csc_m)
        if last:
            kv_c = hc @ w_in[l][:, Q_END:V_END]
        else:
            proj_c = hc @ w_in[l]
            kv_c = proj_c[..., Q_END:V_END]
        k_c, v_c = jnp.split(kv_c, 2, axis=-1)
        s_f, s_b = context_states(ret_heads(k_c), ret_heads(v_c), lg_f, lg_b)

        h = modulate(x, sh_m, sc_m)
        y = token_mixers(h @ w_in[l], conv_w[l], gmlp_ln_g[l], gmlp_ln_b[l], gmlp_ws[l], gmlp_bs[l],
                         lg_f, lg_b, s_f, s_b, rows)
        x = layer_norm(DEEPNORM_ALPHA * x + gt_m * (y @ w_out[l]), ln_g[l, 0], ln_b[l, 0])
        y = sq_relu_mlp(modulate(x, sh_f, sc_f), w_ff1[l], w_ff2[l])
        x = layer_norm(DEEPNORM_ALPHA * x + gt_f * y, ln_g[l, 1], ln_b[l, 1])

        if not last:
            yc = token_mixers(proj_c, conv_w[l], gmlp_ln_g[l], gmlp_ln_b[l], gmlp_ws[l], gmlp_bs[l],
                              lg_f, lg_b, zero_state, zero_state, None)
            xc = layer_norm(DEEPNORM_ALPHA * xc + cgt_m * (yc @ w_out[l]), ln_g[l, 0], ln_b[l, 0])
            yc = sq_relu_mlp(modulate(xc, csh_f, csc_f), w_ff1[l], w_ff2[l])
            xc = layer_norm(DEEPNORM_ALPHA * xc + cgt_f * yc, ln_g[l, 1], ln_b[l, 1])
    return x
```

```python
import contextlib
import os
import numpy as np
import concourse.bass as bass
import concourse.mybir as mybir
from concourse.bass_utils import run_bass_kernel_spmd

F32 = mybir.dt.float32
BF16 = mybir.dt.bfloat16
AF = mybir.ActivationFunctionType
ALU = mybir.AluOpType
AX = mybir.AxisListType

PE, ACT, DVE, POOL, SP = "tensor", "scalar", "vector", "gpsimd", "sync"
ENGS = (PE, ACT, DVE, POOL, SP)

D = 1024
SEQ = 4096
CTXL = 256
DIN = 3328
DFF = 4096
ALPHA = float(4.0 ** 0.25)
EPS = 1e-5
NTM = 4
NTF = 4


class Buf:
    __slots__ = ("name", "w", "rs", "excl")

    def __init__(self, name, excl=False):
        self.name = name
        self.w = None
        self.rs = []
        self.excl = excl


class Op:
    __slots__ = ("eng", "fn", "deps", "needs_inc", "count", "dma_key", "is_dma", "pos")

    def __init__(self, eng, fn, is_dma=False, dma_key=None):
        self.eng = eng
        self.fn = fn
        self.deps = []
        self.needs_inc = False
        self.count = 0
        self.is_dma = is_dma
        self.dma_key = dma_key
        self.pos = 0


class Prog:
    def __init__(self, nc):
        self.nc = nc
        self.streams = {e: [] for e in ENGS}
        self.dma_counts = {}
        self.last_dma = {}

    @staticmethod
    def _collapse(ops):
        last = {}
        out = []
        for r in ops:
            if r.is_dma:
                out.append(r)
            elif r.eng not in last or last[r.eng].pos < r.pos:
                last[r.eng] = r
        return out + list(last.values())

    def _add_deps(self, op, reads, writes):
        deps = op.deps
        for b in reads:
            if b.w is not None:
                deps.append((b.w, "raw"))
            if b.excl:
                b.rs = self._collapse(b.rs)
                for r in b.rs:
                    deps.append((r, "order"))
            b.rs.append(op)
        for b in writes:
            if b.w is not None:
                deps.append((b.w, "order"))
            for r in self._collapse(b.rs):
                if r is not op:
                    deps.append((r, "order"))
            b.w = op
            b.rs = []

    def op(self, eng, fn, reads=(), writes=()):
        o = Op(eng, fn)
        self._add_deps(o, reads, writes)
        o.pos = len(self.streams[eng])
        self.streams[eng].append(o)
        return o

    def dma(self, eng, out, in_, key, reads=(), writes=(), **kw):
        def fn(e, out=out, in_=in_, kw=kw):
            return e.dma_start(out=out, in_=in_, **kw)
        o = Op(eng, fn, is_dma=True, dma_key=key)
        self._add_deps(o, reads, writes)
        if key in self.last_dma:
            o.deps.append((self.last_dma[key], "order"))
        self.last_dma[key] = o
        self.dma_counts[key] = self.dma_counts.get(key, 0) + 1
        o.count = 16 * self.dma_counts[key]
        o.pos = len(self.streams[eng])
        self.streams[eng].append(o)
        return o

    def finalize(self, final_waits=()):
        nc = self.nc
        for e in ENGS:
            for o in self.streams[e]:
                real = []
                for (p, kind) in o.deps:
                    if p.is_dma:
                        real.append(p)
                        continue
                    if (not o.is_dma) and p.eng == o.eng:
                        if e != PE:
                            real.append(p)
                        continue
                    real.append(p)
                o.deps = real
                for p in real:
                    if not p.is_dma:
                        p.needs_inc = True
        for e in ENGS:
            c = 0
            for o in self.streams[e]:
                if o.is_dma:
                    continue
                if o.needs_inc:
                    c += 1
                    o.count = c
        with contextlib.ExitStack() as es:
            esem = {e: es.enter_context(nc.semaphore("s_" + e)) for e in (PE, ACT, DVE, POOL)}
            dsem = {k: es.enter_context(nc.semaphore("d_%s" % (k,))) for k in self.dma_counts}
            block = es.enter_context(nc.Block())

            def emit(e, eng):
                waited = {}
                for o in self.streams[e]:
                    need = {}
                    for p in o.deps:
                        k = ("d", p.dma_key) if p.is_dma else ("e", p.eng)
                        s = dsem[p.dma_key] if p.is_dma else esem[p.eng]
                        if need.get(k, (None, 0))[1] < p.count:
                            need[k] = (s, p.count)
                    for k, (s, v) in need.items():
                        if waited.get(k, 0) >= v:
                            continue
                        waited[k] = v
                        eng.wait_ge(s, v)
                    ins = o.fn(eng)
                    if o.is_dma:
                        ins.then_inc(dsem[o.dma_key], 16)
                    elif o.needs_inc:
                        ins.then_inc(esem[e], 1)
                if e == SP:
                    for p in final_waits:
                        s = dsem[p.dma_key] if p.is_dma else esem[p.eng]
                        eng.wait_ge(s, p.count)

            @block.sync
            def _(eng):
                emit(SP, eng)

            @block.scalar
            def _(eng):
                emit(ACT, eng)

            @block.vector
            def _(eng):
                emit(DVE, eng)

            @block.gpsimd
            def _(eng):
                emit(POOL, eng)

            @block.tensor
            def _(eng):
                emit(PE, eng)


class TB:
    __slots__ = ("ap", "b")

    def __init__(self, ap, name, excl=False):
        self.ap = ap
        self.b = Buf(name, excl)


class Arena:
    def __init__(self, base_ap_f32, nwords):
        self.base = base_ap_f32
        self.nwords = nwords
        self.off = 0

    def reset(self):
        self.off = 0

    def get(self, shape, dt, name="a"):
        n = 1
        for s in shape:
            n *= s
        nbytes = n * (2 if dt == BF16 else 4)
        nw = (nbytes + 3) // 4
        nw = (nw + 7) // 8 * 8
        assert self.off + nw <= self.nwords, ("arena overflow", name, self.off, nw, self.nwords)
        ap = self.base[:, self.off:self.off + (nbytes + 3) // 4]
        self.off += nw
        if dt == BF16:
            ap = ap.bitcast(BF16)
        if len(shape) == 2:
            ap = ap.rearrange("p (a b) -> p a b", a=shape[0])
        elif len(shape) == 3:
            ap = ap.rearrange("p (a b c) -> p a b c", a=shape[0], b=shape[1])
        return ap


def build_program(kstop=None):
    nc = bass.Bass("TRN2", target_bir_lowering=False)
    p = Prog(nc)

    def din(name, shape):
        return nc.dram_tensor(name, list(shape), F32, kind="ExternalInput")

    x_d = din("x", [SEQ, D])
    ctx_d = din("ctx", [CTXL, D])
    cc_d = din("cc", [128, 8, 2])
    wada_d = din("w_ada", [2, D, 6 * D])
    bcol_d = din("b_col", [128, 2, 48])
    bada_d = din("b_ada", [2, 6 * D])
    win_d = din("w_in", [2, D, DIN])
    cw_d = din("cw", [128, 2, 2, 3])
    glng_d = din("gln_g", [2, 256])
    glnb_d = din("gln_b", [2, 256])
    wsT_d = din("wsT", [2, 128, 4, 128])
    bsT_d = din("bsT", [128, 2, 4])
    dec_d = din("dec", [128, 32])
    wout_d = din("w_out", [2, D, D])
    wff1_d = din("w_ff1", [2, D, DFF])
    wff2_d = din("w_ff2", [2, DFF, D])
    lng_d = din("ln_g", [2, 2, D])
    lnb_d = din("ln_b", [2, 2, D])
    y_d = nc.dram_tensor("y", [SEQ, D], F32, kind="ExternalOutput")
    x1_d = nc.dram_tensor("x1buf", [SEQ, D], F32)
    x2_d = nc.dram_tensor("x2buf", [SEQ, D], F32)
    xcm_d = nc.dram_tensor("xcm", [CTXL, D], F32)
    xc1_d = nc.dram_tensor("xc1", [CTXL, D], F32)
    hTl_d = nc.dram_tensor("hTl", [8, 128, 8 * 512], BF16)
    hTc_d = nc.dram_tensor("hTc", [1, 128, 8 * 256], BF16)

    def dram_tiles(t, n):
        return [TB(t[i * 128:(i + 1) * 128, :], "%s_%d" % (t.name, i)) for i in range(n)]

    xT = dram_tiles(x_d, 32)
    ctxT = dram_tiles(ctx_d, 2)
    yT_ = dram_tiles(y_d, 32)
    x1T = dram_tiles(x1_d, 32)
    x2T = dram_tiles(x2_d, 32)
    xcmT = dram_tiles(xcm_d, 2)
    xc1T = dram_tiles(xc1_d, 2)
    hTlG = [TB(hTl_d[g, :, :].rearrange("p (k n) -> p k n", k=8), "hTl%d" % g) for g in range(8)]
    hTcG = [TB(hTc_d[0, :, :].rearrange("p (k n) -> p k n", k=8), "hTc0")]

    def sbt(name, shape, dt=F32):
        t = nc.alloc_sbuf_tensor("sb_" + name, list(shape), dt)
        return t

    def full(t):
        nd = len(t.shape)
        return t[tuple(slice(None) for _ in range(nd))]

    banks = []
    for i in range(8):
        t = nc.alloc_psum_tensor("bank%d" % i, [128, 512], F32)
        banks.append(TB(t[:, :], "bank%d" % i, excl=True))
    bank_ctr = [0]

    def nb():
        b = banks[bank_ctr[0] % 8]
        bank_ctr[0] += 1
        return b

    NSLOT = 3
    wslots = []
    for i in range(NSLOT):
        t = sbt("wslot%d" % i, [128, 8192], BF16)
        wslots.append(TB(t[:, :], "wslot%d" % i))
    slot_ctr = [0]

    def load_w(src_ap, shape, cast=True):
        s = wslots[slot_ctr[0] % NSLOT]
        key = ("w%d" if cast else "wf%d") % (slot_ctr[0] % NSLOT)
        slot_ctr[0] += 1
        n = 1
        for v in shape:
            n *= v
        if cast:
            view = s.ap[:, 0:n]
        else:
            view = s.ap.bitcast(F32)[:, 0:n]
        if len(shape) == 2:
            view = view.rearrange("p (a b) -> p a b", a=shape[0])
        p.dma(POOL if cast else SP, view, src_ap, key=key, writes=[s.b])
        return view, s

    xl = []
    xr = []
    for i in range(2):
        t = sbt("xl%d" % i, [128, D])
        xl.append(TB(t[:, :], "xl%d" % i))
        t = sbt("xr%d" % i, [128, D])
        xr.append(TB(t[:, :], "xr%d" % i))
    pending_x = {}

    def prefetch_x(tiles, t_list):
        for t in t_list:
            if t < len(tiles):
                ld(xl[t % 2], tiles[t].ap, "xl%d" % (t % 2), src_tb=tiles[t])
                pending_x[id(tiles[t])] = t % 2

    rel = TB(full(sbt("rel", [128, 128])), "rel")
    ident = TB(full(sbt("ident", [128, 128])), "ident")
    identb = TB(full(sbt("identb", [128, 128], BF16)), "identb")
    iota1 = TB(full(sbt("iota1", [128, 128])), "iota1")
    iotar = TB(full(sbt("iotar", [128, 128])), "iotar")
    pcol = TB(full(sbt("pcol", [128, 1])), "pcol")
    pcolr = TB(full(sbt("pcolr", [128, 1])), "pcolr")
    lgall = TB(full(sbt("lgall", [128, 32])), "lgall")
    DT = TB(full(sbt("DT", [128, 8, 128], BF16)), "DT")
    dqf = TB(full(sbt("dqf", [128, 4, 128])), "dqf")
    dqb = TB(full(sbt("dqb", [128, 4, 128])), "dqb")
    dkf = TB(full(sbt("dkf", [128, 8])), "dkf")
    dkb = TB(full(sbt("dkb", [128, 8])), "dkb")
    lgself = TB(full(sbt("lgself", [128, 4])), "lgself")
    lgselb = TB(full(sbt("lgselb", [128, 4])), "lgselb")
    Gself = TB(full(sbt("Gself", [128, 4])), "Gself")
    Gselb = TB(full(sbt("Gselb", [128, 4])), "Gselb")
    Gtf = TB(full(sbt("Gtf", [128, 4, 64])), "Gtf")
    Gtb = TB(full(sbt("Gtb", [128, 4, 64])), "Gtb")
    tmp8 = TB(full(sbt("tmp8", [128, 8])), "tmp8")
    barr = TB(full(sbt("barr", [128, 1])), "barr")
    modcol = TB(full(sbt("modcol", [128, 48, 2])), "modcol")
    bcol = TB(full(sbt("bcol", [128, 2, 48])), "bcol")
    ccs = TB(full(sbt("ccs", [128, 8, 2])), "ccs")
    scs = TB(full(sbt("scs", [128, 8, 2])), "scs")
    scb = TB(full(sbt("scb", [128, 8, 2], BF16)), "scb")
    cwt = TB(full(sbt("cwt", [128, 2, 2, 3])), "cwt")
    glng = TB(full(sbt("glng", [128, 256])), "glng")
    glnb = TB(full(sbt("glnb", [128, 256])), "glnb")
    wsTb = TB(full(sbt("wsTb", [128, 4, 128], BF16)), "wsTb")
    bsT = TB(full(sbt("bsT", [128, 2, 4])), "bsT")
    SF = TB(full(sbt("SF", [128, 33, 4, 64], BF16)), "SF")
    SB = TB(full(sbt("SB", [128, 33, 4, 64], BF16)), "SB")
    SFc = TB(full(sbt("SFc", [128, 3, 4, 64], BF16)), "SFc")
    SBc = TB(full(sbt("SBc", [128, 3, 4, 64], BF16)), "SBc")
    cinf = TB(full(sbt("cinf", [128, 4, 64])), "cinf")
    cinb = TB(full(sbt("cinb", [128, 4, 64])), "cinb")
    zst = TB(full(sbt("zst", [128, 4, 64])), "zst")

    ARENA_W = 23040
    arena_t = sbt("arena", [128, ARENA_W])
    ar = Arena(arena_t[:, :], ARENA_W)
    phase_tok = [0]

    def V(fn, reads=(), writes=()):
        return p.op(DVE, fn, [r.b for r in reads], [w.b for w in writes])

    def A(fn, reads=(), writes=()):
        return p.op(ACT, fn, [r.b for r in reads], [w.b for w in writes])

    def G(fn, reads=(), writes=()):
        return p.op(POOL, fn, [r.b for r in reads], [w.b for w in writes])

    def T(fn, reads=(), writes=()):
        return p.op(PE, fn, [r.b for r in reads], [w.b for w in writes])

    def ld(dst, src_ap, key, src_tb=None):
        return p.dma(SP, dst.ap, src_ap, key=key, reads=[src_tb.b] if src_tb is not None else [], writes=[dst.b])

    arena_bufs = []

    last_barrier = [None]

    def AT(shape, dt, name):
        tb = TB(ar.get(shape, dt, name), name)
        tb.b.w = last_barrier[0]
        arena_bufs.append(tb)
        return tb

    stop = [False]
    phase_no = [0]

    def phase_switch():
        phase_no[0] += 1
        if kstop is not None and phase_no[0] > kstop:
            stop[0] = True
        if arena_bufs:
            last_barrier[0] = V(lambda e: e.memset(barr.ap, 0.0), writes=list(arena_bufs) + [barr])
        del arena_bufs[:]
        ar.reset()

    G(lambda e: e.iota(rel.ap, [[1, 128]], base=0, channel_multiplier=-1, allow_small_or_imprecise_dtypes=True),
      writes=[rel])
    G(lambda e: e.iota(iota1.ap, [[1, 128]], base=1, channel_multiplier=0, allow_small_or_imprecise_dtypes=True),
      writes=[iota1])
    G(lambda e: e.iota(iotar.ap, [[-1, 128]], base=128, channel_multiplier=0, allow_small_or_imprecise_dtypes=True),
      writes=[iotar])
    G(lambda e: e.iota(pcol.ap, [[0, 1]], base=0, channel_multiplier=1, allow_small_or_imprecise_dtypes=True),
      writes=[pcol])
    G(lambda e: e.iota(pcolr.ap, [[0, 1]], base=127, channel_multiplier=-1, allow_small_or_imprecise_dtypes=True),
      writes=[pcolr])
    V(lambda e: e.tensor_single_scalar(ident.ap, rel.ap, 0.0, op=ALU.is_equal), [rel], [ident])
    V(lambda e: e.tensor_copy(identb.ap, ident.ap), [ident], [identb])
    V(lambda e: e.memset(zst.ap, 0.0), writes=[zst])
    V(lambda e: e.memset(SFc.ap, 0.0), writes=[SFc])
    V(lambda e: e.memset(SBc.ap, 0.0), writes=[SBc])
    ld(lgall, dec_d[:, :], "misc")
    ld(bcol, bcol_d[:, :, :], "misc")
    ld(ccs, cc_d[:, :, :], "misc")
    ld(cwt, cw_d[:, :, :, :], "misc")
    ld(bsT, bsT_d[:, :, :], "misc")
    A(lambda e: e.activation(lgall.ap, lgall.ap, AF.Exp), [lgall], [lgall])
    V(lambda e: e.tensor_scalar_mul(lgall.ap, lgall.ap, -1.0), [lgall], [lgall])
    A(lambda e: e.activation(scs.ap, ccs.ap, AF.Silu), [ccs], [scs])
    V(lambda e: e.tensor_copy(scb.ap, scs.ap), [scs], [scb])

    def layer_tables(l):
        phase_switch()
        if stop[0]:
            return
        relpos = AT([128], F32, "relpos")
        relneg = AT([128], F32, "relneg")
        maskf = AT([128], F32, "maskf")
        maskb = AT([128], F32, "maskb")
        tmpA = AT([128], F32, "tmpA")
        tmpB = AT([128], F32, "tmpB")
        V(lambda e: e.tensor_single_scalar(maskf.ap, rel.ap, 0.0, op=ALU.is_ge), [rel], [maskf])
        V(lambda e: e.tensor_single_scalar(maskb.ap, rel.ap, 0.0, op=ALU.is_lt), [rel], [maskb])
        V(lambda e: e.tensor_scalar_max(relpos.ap, rel.ap, 0.0), [rel], [relpos])
        V(lambda e: e.tensor_sub(relneg.ap, relpos.ap, rel.ap), [relpos, rel], [relneg])
        lgf = lgall.ap[:, l * 8:(l + 1) * 8]
        lgb = lgall.ap[:, 16 + l * 8:16 + (l + 1) * 8]
        for h in range(8):
            A(lambda e, h=h: e.activation(tmpA.ap, relpos.ap, AF.Exp, scale=lgf[:, h:h + 1]), [relpos, lgall], [tmpA])
            A(lambda e, h=h: e.activation(tmpB.ap, relneg.ap, AF.Exp, scale=lgb[:, h:h + 1]), [relneg, lgall], [tmpB])
            V(lambda e: e.tensor_mul(tmpA.ap, tmpA.ap, maskf.ap), [tmpA, maskf], [tmpA])
            V(lambda e: e.tensor_mul(tmpB.ap, tmpB.ap, maskb.ap), [tmpB, maskb], [tmpB])
            V(lambda e, h=h: e.tensor_add(DT.ap[:, h, :], tmpA.ap, tmpB.ap), [tmpA, tmpB], [DT])
        for (sel, lg) in ((lgself, lgf), (lgselb, lgb)):
            V(lambda e, sel=sel, lg=lg: e.tensor_copy(sel.ap[0:64, :], lg[0:64, 0::2]), [lgall], [sel])
            V(lambda e, sel=sel, lg=lg: e.tensor_copy(sel.ap[64:128, :], lg[64:128, 1::2]), [lgall], [sel])
        for pr in range(4):
            A(lambda e, pr=pr: e.activation(tmpA.ap, iota1.ap, AF.Exp, scale=lgself.ap[:, pr:pr + 1]),
              [iota1, lgself], [tmpA])
            V(lambda e, pr=pr: e.tensor_scalar_mul(dqf.ap[:, pr, :], tmpA.ap, 0.125), [tmpA], [dqf])
            A(lambda e, pr=pr: e.activation(tmpB.ap, iotar.ap, AF.Exp, scale=lgselb.ap[:, pr:pr + 1]),
              [iotar, lgselb], [tmpB])
            V(lambda e, pr=pr: e.tensor_scalar_mul(dqb.ap[:, pr, :], tmpB.ap, 0.125), [tmpB], [dqb])
        V(lambda e: e.tensor_scalar_mul(tmp8.ap, lgf, pcolr.ap[:, 0:1]), [lgall, pcolr], [tmp8])
        A(lambda e: e.activation(dkf.ap, tmp8.ap, AF.Exp), [tmp8], [dkf])
        V(lambda e: e.tensor_scalar_mul(tmp8.ap, lgb, pcol.ap[:, 0:1]), [lgall, pcol], [tmp8])
        A(lambda e: e.activation(dkb.ap, tmp8.ap, AF.Exp), [tmp8], [dkb])
        A(lambda e: e.activation(Gself.ap, lgself.ap, AF.Exp, scale=128.0), [lgself], [Gself])
        A(lambda e: e.activation(Gselb.ap, lgselb.ap, AF.Exp, scale=128.0), [lgselb], [Gselb])
        V(lambda e: e.tensor_copy(Gtf.ap, Gself.ap.unsqueeze(2).to_broadcast([128, 4, 64])), [Gself], [Gtf])
        V(lambda e: e.tensor_copy(Gtb.ap, Gselb.ap.unsqueeze(2).to_broadcast([128, 4, 64])), [Gselb], [Gtb])
        ld(glng, glng_d[l:l + 1, :].to_broadcast([128, 256]), "misc")
        ld(glnb, glnb_d[l:l + 1, :].to_broadcast([128, 256]), "misc")
        wsTf = AT([4, 128], F32, "wsTf")
        ld(wsTf, wsT_d[l, :, :, :], "misc")
        V(lambda e: e.tensor_copy(wsTb.ap, wsTf.ap), [wsTf], [wsTb])

    def modulation(l):
        bk = nb()
        wv_all = wada_d[l, :, :].rearrange("(k p) n -> p k n", p=128)
        for pc in range(12):
            wv, ws = load_w(wv_all[:, :, pc * 512:(pc + 1) * 512], [8, 512], cast=True)
            for jj in range(4):
                j = pc * 4 + jj
                for k in range(8):
                    T(lambda e, j=j, jj=jj, k=k, wv=wv: e.matmul(
                        bk.ap[:, 2 * j:2 * j + 2], wv[:, k, jj * 128:(jj + 1) * 128], scb.ap[:, k, :],
                        start=(k == 0), stop=(k == 7)), [ws, scb], [bk])
        V(lambda e: e.tensor_tensor(modcol.ap, bk.ap[:, 0:96].rearrange("p (a b) -> p a b", b=2),
                                    bcol.ap[:, l, :].unsqueeze(2).to_broadcast([128, 48, 2]), op=ALU.add),
          [bk, bcol], [modcol])
        V(lambda e: e.tensor_scalar_add(modcol.ap[:, 8:16, :], modcol.ap[:, 8:16, :], 1.0), [modcol], [modcol])
        V(lambda e: e.tensor_scalar_add(modcol.ap[:, 32:40, :], modcol.ap[:, 32:40, :], 1.0), [modcol], [modcol])

    def gate_table(l, m, which, dst, screp, brow):
        V(lambda e: e.tensor_copy(screp.ap, scs.ap[:, :, which:which + 1].to_broadcast([128, 8, 128])),
          [scs], [screp])
        ld(brow, bada_d[l:l + 1, m * 1024:(m + 1) * 1024].to_broadcast([128, 1024]), "brow")
        wv_all = wada_d[l, :, :].rearrange("(k p) n -> p k n", p=128)
        for hf in range(2):
            c0 = m * 1024 + hf * 512
            wv, ws = load_w(wv_all[:, :, c0:c0 + 512], [8, 512], cast=True)
            bk = nb()
            for k in range(8):
                T(lambda e, k=k, wv=wv, bk=bk: e.matmul(bk.ap, screp.ap[:, k, :], wv[:, k, :],
                                                          start=(k == 0), stop=(k == 7)), [ws, screp], [bk])
            V(lambda e, hf=hf, bk=bk: e.tensor_add(dst.ap[:, hf * 512:(hf + 1) * 512], bk.ap,
                                                    brow.ap[:, hf * 512:(hf + 1) * 512]), [bk, brow], [dst])

    def gelu(bk, dst, tmp=None):
        A(lambda e: e.activation(dst.ap, bk.ap[:, 0:256], AF.Gelu_apprx_tanh), [bk], [dst])

    def row_table(dst, src_row_ap, key):
        ld(dst, src_row_ap.to_broadcast([128, 1024]), key)

    def make_hT(tiles_src, nt, which, m_shift, m_scale, hT, hTb, next_tiles=None, only=None):
        for t in (range(nt) if only is None else only):
            xs = xl[t % 2]
            if pending_x.pop(id(tiles_src[t]), None) is None:
                ld(xs, tiles_src[t].ap, "xl%d" % (t % 2), src_tb=tiles_src[t])
            for kg in range(2):
                bk = nb()
                for kk in range(4):
                    k = kg * 4 + kk
                    T(lambda e, k=k, kk=kk, bk=bk, xs=xs: e.transpose(
                        bk.ap[:, kk * 128:(kk + 1) * 128], xs.ap[:, k * 128:(k + 1) * 128], ident.ap),
                      [xs, ident], [bk])
                for kk in range(4):
                    k = kg * 4 + kk
                    sc = modcol.ap[:, m_scale * 8 + k, which:which + 1]
                    sh = modcol.ap[:, m_shift * 8 + k, which:which + 1]
                    o_ap = hT.ap[:, k, t * 128:(t + 1) * 128]
                    i_ap = bk.ap[:, kk * 128:(kk + 1) * 128]
                    if kk % 2 == 0:
                        A(lambda e, o_ap=o_ap, i_ap=i_ap, sc=sc, sh=sh: e.activation(
                            o_ap, i_ap, AF.Identity, bias=sh, scale=sc), [bk, modcol], [hTb[t][kg]])
                    else:
                        V(lambda e, o_ap=o_ap, i_ap=i_ap, sc=sc, sh=sh: e.tensor_scalar(
                            o_ap, i_ap, sc, sh, op0=ALU.mult, op1=ALU.add), [bk, modcol], [hTb[t][kg]])
        if next_tiles:
            prefetch_x(next_tiles, [0, 1])

    def hT_reads(hTb, tiles, k):
        return [hTb[t][k // 4] for t in tiles]

    class BB:
        def __init__(self, b):
            self.b = b

    def mk_hTb(nt, name):
        r = [[BB(Buf("%s_%d_%d" % (name, t, g))) for g in range(2)] for t in range(nt)]
        for row in r:
            for bb in row:
                bb.b.w = last_barrier[0]
                arena_bufs.append(bb)
        return r

    def pass1(l, tiles_src, nt_total, which, init_f, init_b, SFo, SBo, want_own, hTG):
        phase_switch()
        if stop[0]:
            return zst, zst
        ng = nt_total // min(4, nt_total)
        gsz = min(4, nt_total)
        hT = AT([8, gsz * 128], BF16, "p1_hT")
        dSb = AT([nt_total, 4, 64], F32, "p1_dSb")
        kdf = [AT([512], BF16, "p1_kdf%d" % i) for i in range(2)]
        kdb = [AT([512], BF16, "p1_kdb%d" % i) for i in range(2)]
        vbf = [AT([512], BF16, "p1_v%d" % i) for i in range(2)]
        stf = [AT([4, 64], F32, "p1_stf%d" % i) for i in range(2)]
        stb = [AT([4, 64], F32, "p1_stb%d" % i) for i in range(2)]
        wall = win_d[l, :, :].rearrange("(k p) n -> p k n", p=128)
        wk, wks = load_w(wall[:, :, 1792:2304], [8, 512])
        wvv, wvs = load_w(wall[:, :, 2304:2816], [8, 512])
        cur = 0
        V(lambda e: e.tensor_copy(stf[0].ap, init_f.ap), [init_f], [stf[0]])
        if want_own:
            A(lambda e: e.activation(SFo.ap[:, 0, :, :], stf[0].ap, AF.Copy), [stf[0]], [SFo])
        hTb = mk_hTb(gsz, "p1hT")
        for g in range(ng):
            make_hT(tiles_src[g * gsz:(g + 1) * gsz], gsz, which, 0, 1, hT, hTb,
                    next_tiles=tiles_src[(g + 1) * gsz:(g + 2) * gsz] if g + 1 < ng else None)
            p.dma(POOL, hTG[g].ap, hT.ap, key="hTst", reads=[bb.b for row in hTb for bb in row], writes=[hTG[g].b])
            for t in range(gsz):
                c = g * gsz + t
                i2 = c % 2
                bk_k = nb()
                bk_v = nb()
                for k in range(8):
                    T(lambda e, k=k, t=t, bk_k=bk_k: e.matmul(bk_k.ap, hT.ap[:, k, t * 128:(t + 1) * 128], wk[:, k, :],
                                                                start=(k == 0), stop=(k == 7)),
                      [hTb[t][k // 4], wks], [bk_k])
                for k in range(8):
                    T(lambda e, k=k, t=t, bk_v=bk_v: e.matmul(bk_v.ap, hT.ap[:, k, t * 128:(t + 1) * 128], wvv[:, k, :],
                                                                start=(k == 0), stop=(k == 7)),
                      [hTb[t][k // 4], wvs], [bk_v])
                kview = bk_k.ap.rearrange("p (h d) -> p h d", h=8)
                V(lambda e, i2=i2, kview=kview: e.tensor_tensor(
                    kdf[i2].ap.rearrange("p (h d) -> p h d", h=8), kview,
                    dkf.ap.unsqueeze(2).to_broadcast([128, 8, 64]), op=ALU.mult), [bk_k, dkf], [kdf[i2]])
                V(lambda e, i2=i2, kview=kview: e.tensor_tensor(
                    kdb[i2].ap.rearrange("p (h d) -> p h d", h=8), kview,
                    dkb.ap.unsqueeze(2).to_broadcast([128, 8, 64]), op=ALU.mult), [bk_k, dkb], [kdb[i2]])
                A(lambda e, i2=i2, bk_v=bk_v: e.activation(vbf[i2].ap, bk_v.ap, AF.Copy), [bk_v], [vbf[i2]])
                bf_ = nb()
                bb_ = nb()
                for pr in range(4):
                    T(lambda e, pr=pr, i2=i2, bf_=bf_: e.matmul(
                        bf_.ap[:, pr * 128:(pr + 1) * 128], kdf[i2].ap[:, pr * 128:(pr + 1) * 128],
                        vbf[i2].ap[:, pr * 128:(pr + 1) * 128], start=True, stop=True), [kdf[i2], vbf[i2]], [bf_])
                for pr in range(4):
                    T(lambda e, pr=pr, i2=i2, bb_=bb_: e.matmul(
                        bb_.ap[:, pr * 128:(pr + 1) * 128], kdb[i2].ap[:, pr * 128:(pr + 1) * 128],
                        vbf[i2].ap[:, pr * 128:(pr + 1) * 128], start=True, stop=True), [kdb[i2], vbf[i2]], [bb_])
                nxt = 1 - cur
                V(lambda e, cur=cur, nxt=nxt: e.tensor_mul(stf[nxt].ap, stf[cur].ap, Gtf.ap), [stf[cur], Gtf], [stf[nxt]])
                for hl in range(2):
                    ps_ = slice(hl * 64, hl * 64 + 64)
                    dsv = bf_.ap.rearrange("p (a b) -> p a b", a=4)[ps_, :, hl * 64:hl * 64 + 64]
                    V(lambda e, nxt=nxt, ps_=ps_, dsv=dsv: e.tensor_add(stf[nxt].ap[ps_, :, :], stf[nxt].ap[ps_, :, :], dsv),
                      [stf[nxt], bf_], [stf[nxt]])
                    dsv2 = bb_.ap.rearrange("p (a b) -> p a b", a=4)[ps_, :, hl * 64:hl * 64 + 64]
                    A(lambda e, c=c, ps_=ps_, dsv2=dsv2: e.activation(dSb.ap[ps_, c, :, :], dsv2, AF.Copy), [bb_], [dSb])
                cur = nxt
                if want_own or True:
                    if c + 1 < nt_total or True:
                        A(lambda e, c=c, cur=cur: e.activation(SFo.ap[:, c + 1, :, :], stf[cur].ap, AF.Copy),
                          [stf[cur]], [SFo])
        fin_f = stf[cur]
        curb = 0
        V(lambda e: e.tensor_copy(stb[0].ap, init_b.ap), [init_b], [stb[0]])
        A(lambda e: e.activation(SBo.ap[:, nt_total - 1, :, :], stb[0].ap, AF.Copy), [stb[0]], [SBo])
        for c in range(nt_total - 1, -1, -1):
            nxt = 1 - curb
            V(lambda e, curb=curb, nxt=nxt: e.tensor_mul(stb[nxt].ap, stb[curb].ap, Gtb.ap), [stb[curb], Gtb], [stb[nxt]])
            V(lambda e, nxt=nxt, c=c: e.tensor_add(stb[nxt].ap, stb[nxt].ap, dSb.ap[:, c, :, :]), [stb[nxt], dSb], [stb[nxt]])
            curb = nxt
            idx = c - 1 if c > 0 else nt_total
            A(lambda e, idx=idx, curb=curb: e.activation(SBo.ap[:, idx, :, :], stb[curb].ap, AF.Copy), [stb[curb]], [SBo])
        fin_b = stb[curb]
        return fin_f, fin_b

    def layer_norm_store(l, which_ln, bkA, bkB, xres, gt, lng_t, lnb_t, tbuf, stat, dst_tile, key, store_eng=None):
        for hf, bk in enumerate((bkA, bkB)):
            sl = slice(hf * 512, (hf + 1) * 512)
            V(lambda e, bk=bk, sl=sl: e.tensor_mul(tbuf.ap[:, sl], bk.ap, gt.ap[:, sl]), [bk, gt], [tbuf])
        ln_tail(xres, tbuf, lng_t, lnb_t, stat, dst_tile, key, store_eng=store_eng)

    def ln_tail(xres, tbuf, lng_t, lnb_t, stat, dst_tile, key, aff=None, store_eng=None):
        V(lambda e: e.scalar_tensor_tensor(tbuf.ap, xres.ap, ALPHA, tbuf.ap, op0=ALU.mult, op1=ALU.add),
          [xres, tbuf], [tbuf])
        for hf in range(2):
            V(lambda e, hf=hf: e.bn_stats(stat.ap[:, hf * 6:(hf + 1) * 6], tbuf.ap[:, hf * 512:(hf + 1) * 512]),
              [tbuf], [stat])
        V(lambda e: e.bn_aggr(stat.ap[:, 12:14], stat.ap[:, 0:12]), [stat], [stat])
        A(lambda e: e.activation(stat.ap[:, 14:15], stat.ap[:, 13:14], AF.Sqrt, bias=epsc.ap[:, 0:1]), [stat, epsc], [stat])
        V(lambda e: e.reciprocal(stat.ap[:, 15:16], stat.ap[:, 14:15]), [stat], [stat])
        V(lambda e: e.scalar_tensor_tensor(stat.ap[:, 16:17], stat.ap[:, 12:13], -1.0, stat.ap[:, 15:16],
                                           op0=ALU.mult, op1=ALU.mult), [stat], [stat])
        A(lambda e: e.activation(tbuf.ap, tbuf.ap, AF.Identity, bias=stat.ap[:, 16:17], scale=stat.ap[:, 15:16]),
          [tbuf, stat], [tbuf])
        aff = aff or V
        aff(lambda e: e.tensor_mul(tbuf.ap, tbuf.ap, lng_t.ap), [tbuf, lng_t], [tbuf])
        aff(lambda e: e.tensor_add(tbuf.ap, tbuf.ap, lnb_t.ap), [tbuf, lnb_t], [tbuf])
        return p.dma(store_eng or ACT, dst_tile.ap, tbuf.ap, key=key, reads=[tbuf.b], writes=[dst_tile.b])

    epsc = TB(full(sbt("epsc", [128, 1])), "epsc")
    V(lambda e: e.memset(epsc.ap, EPS), writes=[epsc])

    def mixer(l, tiles_src, tiles_dst, nt_total, which, SFo, SBo, rowlen, hTG):
        phase_switch()
        if stop[0]:
            return
        gsz = min(NTM, nt_total)
        ng = nt_total // gsz
        NTOK = gsz * 128
        gt = AT([1024], F32, "m_gt")
        lng_t = AT([1024], F32, "m_lng")
        lnb_t = AT([1024], F32, "m_lnb")
        hT = AT([8, NTOK], BF16, "m_hT")
        qT = AT([4, NTOK], BF16, "m_qT")
        qdfT = AT([4, NTOK], BF16, "m_qdf")
        qdbT = AT([4, NTOK], BF16, "m_qdb")
        kT = AT([4, NTOK], BF16, "m_kT")
        yT = AT([8, NTOK], BF16, "m_yT")
        xin = [AT([max(NTOK, 512)], F32, "m_xin%d" % i) for i in range(2)]
        zc = [AT([max(NTOK, 512)], F32, "m_z%d" % i) for i in range(2)]
        yc = xin
        gtmp = None
        ug = [AT([256], F32, "m_ug%d" % i) for i in range(gsz)]
        vgl = [AT([256], F32, "m_vg%d" % i) for i in range(2)]
        vln = [AT([256], BF16, "m_vln%d" % i) for i in range(gsz)]
        vrb = [AT([512], BF16, "m_vr%d" % i) for i in range(gsz)]
        sg = [AT([512], F32, "m_sg%d" % i) for i in range(gsz)]
        PT = [[AT([4, 128], BF16, "m_PT%d_%d" % (i, j)) for j in range(2)] for i in range(2)]
        ygm = [AT([256], F32, "m_ygm%d" % i) for i in range(2)]
        tb = [AT([1024], F32, "m_tb%d" % i) for i in range(2)]
        stat = [AT([24], F32, "m_stat%d" % i) for i in range(2)]
        gstl = [AT([48], F32, "m_gst%d" % i) for i in range(4)]
        gst = gstl[0]
        osbl, osql = [], []
        for i in range(2):
            o_ = TB(xin[i].ap[:, 0:512], "m_osb%d" % i)
            o_.b = xin[i].b
            q_ = TB(zc[i].ap[:, 0:512], "m_osq%d" % i)
            q_.b = zc[i].b
            osbl.append(o_)
            osql.append(q_)
        screp = TB(tb[1].ap.bitcast(BF16)[:, 0:1024].rearrange("p (a b) -> p a b", a=8), "m_screp")
        screp.b = tb[1].b
        gate_table(l, 2, which, gt, screp, tb[0])
        row_table(lng_t, lng_d[l, 0:1, :], "lng")
        row_table(lnb_t, lnb_d[l, 0:1, :], "lnb")
        wall = win_d[l, :, :].rearrange("(k p) n -> p k n", p=128)
        woall = wout_d[l, :, :].rearrange("(k p) n -> p k n", p=128)
        nrow = NTOK // rowlen

        hTb = mk_hTb(gsz, "mhT")
        w0_pre = [(load_w(wall[:, :, 0:1024], [8, 1024]), load_w(wall[:, :, 1024:2048], [8, 1024]))]
        for g in range(ng):
            if g == 0:
                p.dma(SP, hT.ap, hTG[0].ap, key="hTld", reads=[hTG[0].b], writes=[bb.b for row in hTb for bb in row])
            alltiles = list(range(gsz))

            def fm(wv, ws, col, bk):
                for k in range(8):
                    T(lambda e, k=k, wv=wv, col=col, bk=bk: e.matmul(
                        bk.ap[:, 0:NTOK], wv[:, k, col:col + 128], hT.ap[:, k, :], start=(k == 0), stop=(k == 7)),
                      [ws] + hT_reads(hTb, alltiles, k), [bk])

            def tm(wv, ws, col, n, t, bk, off):
                for k in range(8):
                    T(lambda e, k=k, wv=wv, col=col, bk=bk, t=t, n=n, off=off: e.matmul(
                        bk.ap[:, off:off + n], hT.ap[:, k, t * 128:(t + 1) * 128], wv[:, k, col:col + n],
                        start=(k == 0), stop=(k == 7)), [ws, hTb[t][k // 4]], [bk])

            MS = int(os.environ.get("MSTOP", "99"))
            if MS < 1:
                return
            (w0, s0), (w1, s1) = w0_pre[0]
            for c2 in range(2):
                bx = nb()
                fm(w0, s0, c2 * 128, bx)
                A(lambda e, c2=c2, bx=bx: e.activation(xin[c2].ap[:, 0:NTOK], bx.ap[:, 0:NTOK], AF.Copy), [bx], [xin[c2]])
                bc = nb()
                fm(w0, s0, 512 + c2 * 128, bc)
                V(lambda e, c2=c2, bc=bc: e.tensor_mul(zc[c2].ap[:, 0:NTOK], bc.ap[:, 0:NTOK], xin[c2].ap[:, 0:NTOK]), [bc, xin[c2]], [zc[c2]])
                cwv = cwt.ap[:, l, c2, :]
                z3 = zc[c2].ap[:, 0:NTOK].rearrange("p (r w) -> p r w", w=rowlen)
                y3 = yc[c2].ap[:, 0:NTOK].rearrange("p (r w) -> p r w", w=rowlen)
                A(lambda e, c2=c2, cwv=cwv: e.activation(yc[c2].ap[:, 0:NTOK], zc[c2].ap[:, 0:NTOK], AF.Identity, scale=cwv[:, 1:2]),
                  [zc[c2], cwt], [yc[c2]])
                V(lambda e, c2=c2, cwv=cwv, z3=z3, y3=y3: e.scalar_tensor_tensor(
                    y3[:, :, 1:rowlen], z3[:, :, 0:rowlen - 1], cwv[:, 0:1], y3[:, :, 1:rowlen],
                    op0=ALU.mult, op1=ALU.add), [zc[c2], cwt, yc[c2]], [yc[c2]])
                V(lambda e, c2=c2, cwv=cwv, z3=z3, y3=y3: e.scalar_tensor_tensor(
                    y3[:, :, 0:rowlen - 1], z3[:, :, 1:rowlen], cwv[:, 2:3], y3[:, :, 0:rowlen - 1],
                    op0=ALU.mult, op1=ALU.add), [zc[c2], cwt, yc[c2]], [yc[c2]])
                bg = nb()
                fm(w0, s0, 256 + c2 * 128, bg)
                V(lambda e, c2=c2, bg=bg: e.tensor_mul(yT.ap[:, c2, :], bg.ap[:, 0:NTOK], yc[c2].ap[:, 0:NTOK]), [bg, yc[c2]], [yT])
            for t in range(gsz):
                bu = nb()
                tm(w0, s0, 768, 256, t, bu, 0)
                gelu(bu, ug[t], gtmp)
            if MS < 2:
                return
            for t in range(gsz):
                bv = nb()
                tm(w1, s1, 0, 256, t, bv, 0)
                vg = vgl[t % 2]
                gelu(bv, vg, gtmp)
                gs = gstl[t % 4]
                V(lambda e, vg=vg, gs=gs: e.bn_stats(gs.ap[:, 32:38], vg.ap), [vg], [gs])
                V(lambda e, gs=gs: e.bn_aggr(gs.ap[:, 38:40], gs.ap[:, 32:38]), [gs], [gs])
                A(lambda e, gs=gs: e.activation(gs.ap[:, 40:41], gs.ap[:, 39:40], AF.Sqrt, bias=epsc.ap[:, 0:1]), [gs, epsc], [gs])
                V(lambda e, gs=gs: e.reciprocal(gs.ap[:, 41:42], gs.ap[:, 40:41]), [gs], [gs])
                V(lambda e, vg=vg, gs=gs: e.tensor_scalar(vg.ap, vg.ap, gs.ap[:, 38:39], gs.ap[:, 41:42],
                                                          op0=ALU.subtract, op1=ALU.mult), [vg, gs], [vg])
                V(lambda e, vg=vg: e.tensor_mul(vg.ap, vg.ap, glng.ap), [vg, glng], [vg])
                V(lambda e, t=t, vg=vg: e.tensor_add(vln[t].ap, vg.ap, glnb.ap), [vg, glnb], [vln[t]])
            for c4 in range(4):
                bq = nb()
                fm(w1, s1, 256 + c4 * 128, bq)
                A(lambda e, c4=c4, bq=bq: e.activation(qT.ap[:, c4, :], bq.ap[:, 0:NTOK], AF.Copy, scale=0.125), [bq], [qT])
                V(lambda e, c4=c4, bq=bq: e.tensor_tensor(
                    qdfT.ap[:, c4, :].rearrange("p (t i) -> p t i", i=128), bq.ap[:, 0:NTOK].rearrange("p (t i) -> p t i", i=128),
                    dqf.ap[:, c4, :].unsqueeze(1).to_broadcast([128, gsz, 128]), op=ALU.mult), [bq, dqf], [qdfT])
                V(lambda e, c4=c4, bq=bq: e.tensor_tensor(
                    qdbT.ap[:, c4, :].rearrange("p (t i) -> p t i", i=128), bq.ap[:, 0:NTOK].rearrange("p (t i) -> p t i", i=128),
                    dqb.ap[:, c4, :].unsqueeze(1).to_broadcast([128, gsz, 128]), op=ALU.mult), [bq, dqb], [qdbT])
            for c4 in range(2):
                bkk = nb()
                fm(w1, s1, 768 + c4 * 128, bkk)
                A(lambda e, c4=c4, bkk=bkk: e.activation(kT.ap[:, c4, :], bkk.ap[:, 0:NTOK], AF.Copy), [bkk], [kT])
            if MS < 3:
                return
            w2, s2 = load_w(wall[:, :, 2048:3072], [8, 1024])
            for c4 in range(2, 4):
                bkk = nb()
                fm(w2, s2, (c4 - 2) * 128, bkk)
                A(lambda e, c4=c4, bkk=bkk: e.activation(kT.ap[:, c4, :], bkk.ap[:, 0:NTOK], AF.Copy), [bkk], [kT])
            for t in range(gsz):
                bvr = nb()
                tm(w2, s2, 256, 512, t, bvr, 0)
                A(lambda e, t=t, bvr=bvr: e.activation(vrb[t].ap, bvr.ap, AF.Copy), [bvr], [vrb[t]])
            w3, s3 = load_w(wall[:, :, 3072:3328], [8, 256])
            for t in range(gsz):
                bgt = nb()
                tm(w2, s2, 768, 256, t, bgt, 0)
                tm(w3, s3, 0, 256, t, bgt, 256)
                A(lambda e, t=t, bgt=bgt: e.activation(sg[t].ap, bgt.ap, AF.Silu), [bgt], [sg[t]])
            if MS < 4:
                return
            wo, so = load_w(woall[:, :, :], [8, 1024])
            if g + 1 < ng:
                w0_pre[0] = (load_w(wall[:, :, 0:1024], [8, 1024]), load_w(wall[:, :, 1024:2048], [8, 1024]))
            if g + 1 < ng:
                p.dma(SP, hT.ap, hTG[g + 1].ap, key="hTld", reads=[hTG[g + 1].b],
                      writes=[bb.b for row in hTb for bb in row])
            xrm = [xr[0], xr[1], xl[0], xl[1]]
            for t in range(gsz):
                ld(xrm[t], tiles_src[g * gsz + t].ap, "mxr%d" % t, src_tb=tiles_src[g * gsz + t])
            st = [dict() for _ in range(gsz)]

            def stA(t):
                c = g * gsz + t
                tsl = slice(t * 128, (t + 1) * 128)
                d = st[t]
                bm = nb()
                for h in range(4):
                    T(lambda e, h=h, t=t, bm=bm: e.matmul(bm.ap[:, h * 64:(h + 1) * 64], wsTb.ap[:, h, :],
                                                           vln[t].ap[:, h * 64:(h + 1) * 64], start=True, stop=True),
                      [wsTb, vln[t]], [bm])
                bsA, bsB = nb(), nb()
                for h in range(8):
                    pr, hl = h // 2, h % 2
                    ps_ = slice(hl * 64, hl * 64 + 64)
                    bs_ = bsA if hl == 0 else bsB
                    T(lambda e, pr=pr, ps_=ps_, tsl=tsl, bs_=bs_: e.matmul(
                        bs_.ap[:, pr * 128:(pr + 1) * 128], kT.ap[ps_, pr, tsl], qT.ap[ps_, pr, tsl],
                        start=True, stop=True), [kT, qT], [bs_])
                yg = ygm[t % 2]
                for h in range(4):
                    V(lambda e, h=h, t=t, bm=bm, yg=yg: e.scalar_tensor_tensor(
                        yg.ap[:, h * 64:(h + 1) * 64], bm.ap[:, h * 64:(h + 1) * 64], bsT.ap[:, l, h:h + 1],
                        ug[t].ap[:, h * 64:(h + 1) * 64], op0=ALU.add, op1=ALU.mult), [bm, bsT, ug[t]], [yg])
                for hl, bs_ in enumerate((bsA, bsB)):
                    pt = PT[t % 2][hl]
                    V(lambda e, hl=hl, bs_=bs_, pt=pt: e.tensor_tensor(
                        pt.ap[:, 0:4, :], bs_.ap.rearrange("p (a b) -> p a b", a=4), DT.ap[:, hl::2, :],
                        op=ALU.mult), [bs_, DT], [pt])

            def stB(t):
                c = g * gsz + t
                tsl = slice(t * 128, (t + 1) * 128)
                yg = ygm[t % 2]
                btr = nb()
                btv = btr.ap
                for j in range(2):
                    T(lambda e, j=j, btv=btv, yg=yg: e.transpose(btv[:, j * 128:(j + 1) * 128], yg.ap[:, j * 128:(j + 1) * 128],
                                                                  ident.ap), [yg, ident], [btr])
                bo = nb()
                st[t]["bo"] = bo
                for h in range(8):
                    pr, hl = h // 2, h % 2
                    ps_ = slice(hl * 64, hl * 64 + 64)
                    osl = slice(h * 64, (h + 1) * 64)
                    pt = PT[t % 2][hl]
                    T(lambda e, pr=pr, t=t, osl=osl, pt=pt, bo=bo: e.matmul(
                        bo.ap[:, osl], pt.ap[:, pr, :], vrb[t].ap[:, osl], start=True, stop=False), [pt, vrb[t]], [bo])
                    T(lambda e, pr=pr, ps_=ps_, tsl=tsl, osl=osl, c=c, bo=bo: e.matmul(
                        bo.ap[:, osl], qdfT.ap[ps_, pr, tsl], SFo.ap[ps_, c, pr, :], start=False, stop=False),
                      [qdfT, SFo], [bo])
                    T(lambda e, pr=pr, ps_=ps_, tsl=tsl, osl=osl, c=c, bo=bo: e.matmul(
                        bo.ap[:, osl], qdbT.ap[ps_, pr, tsl], SBo.ap[ps_, c, pr, :], start=False, stop=True),
                      [qdbT, SBo], [bo])
                A(lambda e, tsl=tsl, btv=btv: e.activation(
                    yT.ap[:, 2:4, tsl], btv[:, 0:256].rearrange("p (a b) -> p a b", a=2), AF.Copy), [btr], [yT])
                osb, osq = osbl[t % 2], osql[t % 2]
                A(lambda e, bo=bo, osb=osb: e.activation(osb.ap, bo.ap, AF.Copy), [bo], [osb])
                A(lambda e, bo=bo, osq=osq: e.activation(osq.ap, bo.ap, AF.Square), [bo], [osq])

            def stC(t):
                osb, osq = osbl[t % 2], osql[t % 2]
                gs = gstl[t % 4]
                o3 = osb.ap.rearrange("p (h e) -> p h e", h=8)
                q3 = osq.ap.rearrange("p (h e) -> p h e", h=8)
                V(lambda e: e.tensor_reduce(gs.ap[:, 0:8], o3, axis=AX.X, op=ALU.add), [osb], [gs])
                V(lambda e: e.tensor_reduce(gs.ap[:, 8:16], q3, axis=AX.X, op=ALU.add), [osq], [gs])
                V(lambda e: e.tensor_scalar_mul(gs.ap[:, 0:8], gs.ap[:, 0:8], 1.0 / 64), [gs], [gs])
                V(lambda e: e.tensor_mul(gs.ap[:, 16:24], gs.ap[:, 0:8], gs.ap[:, 0:8]), [gs], [gs])
                V(lambda e: e.scalar_tensor_tensor(gs.ap[:, 8:16], gs.ap[:, 8:16], 1.0 / 64, gs.ap[:, 16:24],
                                                   op0=ALU.mult, op1=ALU.subtract), [gs], [gs])
                A(lambda e: e.activation(gs.ap[:, 16:24], gs.ap[:, 8:16], AF.Sqrt, bias=epsc.ap[:, 0:1]), [gs, epsc], [gs])
                V(lambda e: e.reciprocal(gs.ap[:, 24:32], gs.ap[:, 16:24]), [gs], [gs])
                V(lambda e: e.scalar_tensor_tensor(gs.ap[:, 8:16], gs.ap[:, 0:8], -1.0, gs.ap[:, 24:32],
                                                   op0=ALU.mult, op1=ALU.mult), [gs], [gs])
                for h in range(8):
                    A(lambda e, h=h: e.activation(o3[:, h, :], o3[:, h, :], AF.Identity, bias=gs.ap[:, 8 + h:9 + h],
                                                  scale=gs.ap[:, 24 + h:25 + h]), [osb, gs], [osb])
                V(lambda e: e.tensor_mul(osq.ap, osb.ap, sg[t].ap), [osb, sg[t]], [osq])

            def stD(t):
                tsl = slice(t * 128, (t + 1) * 128)
                yrt = osql[t % 2]
                btr2 = nb()
                btv2 = btr2.ap
                for j in range(4):
                    T(lambda e, j=j, btv2=btv2: e.transpose(btv2[:, j * 128:(j + 1) * 128], yrt.ap[:, j * 128:(j + 1) * 128],
                                                             ident.ap), [yrt, ident], [btr2])
                A(lambda e, tsl=tsl, btv2=btv2: e.activation(
                    yT.ap[:, 4:8, tsl], btv2[:, 0:512].rearrange("p (a b) -> p a b", a=4), AF.Copy), [btr2], [yT])

            def stE(t):
                tsl = slice(t * 128, (t + 1) * 128)
                bA, bB = nb(), nb()
                st[t]["bAB"] = (bA, bB)
                for hf, bk in enumerate((bA, bB)):
                    for k in range(8):
                        T(lambda e, k=k, hf=hf, bk=bk, tsl=tsl, wo=wo: e.matmul(
                            bk.ap, yT.ap[:, k, tsl], wo[:, k, hf * 512:(hf + 1) * 512], start=(k == 0), stop=(k == 7)),
                          [yT, so], [bk])

            def stF(t):
                c = g * gsz + t
                bA, bB = st[t]["bAB"]
                layer_norm_store(l, 0, bA, bB, xrm[t], gt, lng_t, lnb_t, tb[t % 2], stat[t % 2],
                                 tiles_dst[c], "st%d" % (t % 2), store_eng=SP)

            stages = [stA, stB, stC, stD, stE, stF]
            for step in range(gsz + len(stages) - 1):
                for si in range(len(stages) - 1, -1, -1):
                    t = step - si
                    if 0 <= t < gsz:
                        stages[si](t)

    def ffn(l, tiles_src, tiles_dst, nt_total, which):
        phase_switch()
        if stop[0]:
            return
        gsz = min(NTF, nt_total)
        ng = nt_total // gsz
        NTOK = gsz * 128
        gt = AT([1024], F32, "f_gt")
        lng_t = AT([1024], F32, "f_lng")
        lnb_t = AT([1024], F32, "f_lnb")
        hTs = [AT([8, NTOK], BF16, "f_hT%d" % i) for i in range(2)]
        aT = AT([32, NTOK], BF16, "f_aT")
        t1 = [AT([1024], F32, "f_t1_%d" % i) for i in range(gsz)]
        rl = [AT([NTOK], F32, "f_rl%d" % i) for i in range(2)]
        stat = [AT([24], F32, "f_stat%d" % i) for i in range(2)]
        xr4 = [xr[0], xr[1]] + [AT([1024], F32, "f_xr%d" % i) for i in range(2, gsz)]
        screp = TB(t1[1].ap.bitcast(BF16)[:, 0:1024].rearrange("p (a b) -> p a b", a=8), "f_screp")
        screp.b = t1[1].b
        gate_table(l, 5, which, gt, screp, t1[0])
        row_table(lng_t, lng_d[l, 1:2, :], "lng")
        row_table(lnb_t, lnb_d[l, 1:2, :], "lnb")
        w1all = wff1_d[l, :, :].rearrange("(k p) n -> p k n", p=128)
        w2all = wff2_d[l, :, :].rearrange("(k p) n -> p k n", p=128)
        hTbs = [mk_hTb(gsz, "fhT%d" % i) for i in range(2)]
        alltiles = list(range(gsz))

        def grp(g):
            return tiles_src[g * gsz:(g + 1) * gsz]

        make_hT(grp(0), gsz, which, 3, 4, hTs[0], hTbs[0], next_tiles=grp(1) if ng > 1 else None)
        for g in range(ng):
            hT, hTb = hTs[g % 2], hTbs[g % 2]
            for pc in range(4):
                wv, ws = load_w(w1all[:, :, pc * 1024:(pc + 1) * 1024], [8, 1024])
                for j in range(8):
                    fc = pc * 8 + j
                    bk = nb()
                    for k in range(8):
                        T(lambda e, k=k, j=j, wv=wv, bk=bk, hT=hT: e.matmul(
                            bk.ap[:, 0:NTOK], wv[:, k, j * 128:(j + 1) * 128], hT.ap[:, k, :], start=(k == 0), stop=(k == 7)),
                          [ws] + hT_reads(hTb, alltiles, k), [bk])
                    r = rl[fc % 2]
                    A(lambda e, bk=bk, r=r: e.activation(r.ap, bk.ap[:, 0:NTOK], AF.Relu), [bk], [r])
                    A(lambda e, fc=fc, r=r: e.activation(aT.ap[:, fc, :], r.ap, AF.Square), [r], [aT])
                if g + 1 < ng and pc < gsz:
                    make_hT(grp(g + 1), gsz, which, 3, 4, hTs[(g + 1) % 2], hTbs[(g + 1) % 2], only=[pc],
                            next_tiles=(grp(g + 2) if (pc == gsz - 1 and g + 2 < ng) else None))
            for t in range(gsz):
                c = g * gsz + t
                ld(xr4[t], tiles_src[c].ap, "fxr%d" % t, src_tb=tiles_src[c])
            for q in range(4):
                wv, ws = load_w(w2all[:, :, q * 256:(q + 1) * 256], [32, 256])
                for t in range(gsz):
                    bk = nb()
                    for k in range(32):
                        T(lambda e, k=k, t=t, wv=wv, bk=bk: e.matmul(
                            bk.ap[:, 0:256], aT.ap[:, k, t * 128:(t + 1) * 128], wv[:, k, :], start=(k == 0), stop=(k == 31)),
                          [ws, aT], [bk])
                    V(lambda e, q=q, t=t, bk=bk: e.tensor_mul(t1[t].ap[:, q * 256:(q + 1) * 256], bk.ap[:, 0:256],
                                                               gt.ap[:, q * 256:(q + 1) * 256]), [bk, gt], [t1[t]])
                    if q == 3:
                        c = g * gsz + t
                        ln_tail(xr4[t], t1[t], lng_t, lnb_t, stat[t % 2], tiles_dst[c], "fst%d" % t, aff=V, store_eng=SP)

    lat_src = xT
    ctx_src = ctxT
    last_out = None
    for l in range(2):
        last = (l == 1)
        layer_tables(l)
        if not stop[0]:
            modulation(l)
        cf, cb = pass1(l, ctx_src, 2, 1, zst, zst, SFc, SBc, True, hTcG)
        V(lambda e, cf=cf: e.tensor_copy(cinf.ap, cf.ap), [cf], [cinf])
        V(lambda e, cb=cb: e.tensor_copy(cinb.ap, cb.ap), [cb], [cinb])
        if not last:
            mixer(l, ctx_src, xcmT, 2, 1, SFc, SBc, 256, hTcG)
            ffn(l, xcmT, xc1T, 2, 1)
        pass1(l, lat_src, 32, 0, cinf, cinb, SF, SB, True, hTlG)
        mixer(l, lat_src, x1T, 32, 0, SF, SB, 64, hTlG)
        dst = yT_ if last else x2T
        ffn(l, x1T, dst, 32, 0)
        lat_src = x2T
        ctx_src = xc1T
    finals = [dst_t.b.w for dst_t in yT_ if dst_t.b.w is not None]
    if stop[0]:
        for lst in (x1T, x2T, xcmT, xc1T):
            finals += [t.b.w for t in lst if t.b.w is not None and t.b.w.is_dma]
    p.finalize(final_waits=finals)
    return nc


_CACHE = {}


def _prep_inputs(inp, b):
    f = np.float32
    c = np.asarray(inp["c"], f)[b]
    cctx = np.asarray(inp["c_ctx"], f)
    cc = np.stack([c.reshape(8, 128).T, cctx.reshape(8, 128).T], axis=-1)
    b_ada = np.asarray(inp["b_ada"], f)
    b_col = np.ascontiguousarray(b_ada.reshape(2, 48, 128).transpose(2, 0, 1))
    conv_w = np.asarray(inp["conv_w"], f)
    cw = np.ascontiguousarray(conv_w.reshape(2, 3, 2, 128).transpose(3, 0, 2, 1))
    ws = np.asarray(inp["gmlp_ws"], f)
    wsT = np.ascontiguousarray(ws.transpose(0, 3, 1, 2))
    bs = np.asarray(inp["gmlp_bs"], f)
    bsT = np.ascontiguousarray(bs.transpose(2, 0, 1))
    dec = np.concatenate([np.asarray(inp["ret_decay_fwd"], f).reshape(-1),
                          np.asarray(inp["ret_decay_bwd"], f).reshape(-1)])
    dec = np.ascontiguousarray(np.broadcast_to(dec[None, :], (128, 32)))
    return {
        "x": np.ascontiguousarray(np.asarray(inp["x"], f)[b]),
        "ctx": np.ascontiguousarray(np.asarray(inp["ctx"], f)[b]),
        "cc": np.ascontiguousarray(cc),
        "w_ada": np.asarray(inp["w_ada"], f),
        "b_col": b_col,
        "b_ada": b_ada,
        "w_in": np.asarray(inp["w_in"], f),
        "cw": cw,
        "gln_g": np.asarray(inp["gmlp_ln_g"], f),
        "gln_b": np.asarray(inp["gmlp_ln_b"], f),
        "wsT": wsT,
        "bsT": bsT,
        "dec": dec,
        "w_out": np.asarray(inp["w_out"], f),
        "w_ff1": np.asarray(inp["w_ff1"], f),
        "w_ff2": np.asarray(inp["w_ff2"], f),
        "ln_g": np.asarray(inp["ln_g"], f),
        "ln_b": np.asarray(inp["ln_b"], f),
    }


def kernel(**inputs):
    if "nc" not in _CACHE:
        _CACHE["nc"] = build_program()
    nc = _CACHE["nc"]
    in_maps = [_prep_inputs(inputs, b) for b in range(8)]
    res = run_bass_kernel_spmd(nc, in_maps, core_ids=list(range(8)))
    return np.stack([np.asarray(r["y"], np.float32) for r in res.results], axis=0)
```

```python
import contextlib
import os
import numpy as np
import concourse.bass as bass
import concourse.mybir as mybir
from concourse.bass_utils import run_bass_kernel_spmd

F32 = mybir.dt.float32
BF16 = mybir.dt.bfloat16
AF = mybir.ActivationFunctionType
ALU = mybir.AluOpType
AX = mybir.AxisListType

PE, ACT, DVE, POOL, SP = "tensor", "scalar", "vector", "gpsimd", "sync"
ENGS = (PE, ACT, DVE, POOL, SP)

D = 1024
SEQ = 4096
CTXL = 256
DIN = 3328
DFF = 4096
ALPHA = float(4.0 ** 0.25)
EPS = 1e-5
NTM = 4
NTF = 4


class Buf:
    __slots__ = ("name", "w", "rs", "excl")

    def __init__(self, name, excl=False):
        self.name = name
        self.w = None
        self.rs = []
        self.excl = excl


class Op:
    __slots__ = ("eng", "fn", "deps", "needs_inc", "count", "dma_key", "is_dma", "pos")

    def __init__(self, eng, fn, is_dma=False, dma_key=None):
        self.eng = eng
        self.fn = fn
        self.deps = []
        self.needs_inc = False
        self.count = 0
        self.is_dma = is_dma
        self.dma_key = dma_key
        self.pos = 0


class Prog:
    def __init__(self, nc):
        self.nc = nc
        self.streams = {e: [] for e in ENGS}
        self.dma_counts = {}
        self.last_dma = {}

    @staticmethod
    def _collapse(ops):
        last = {}
        out = []
        for r in ops:
            if r.is_dma:
                out.append(r)
            elif r.eng not in last or last[r.eng].pos < r.pos:
                last[r.eng] = r
        return out + list(last.values())

    def _add_deps(self, op, reads, writes):
        deps = op.deps
        for b in reads:
            if b.w is not None:
                deps.append((b.w, "raw"))
            if b.excl:
                b.rs = self._collapse(b.rs)
                for r in b.rs:
                    deps.append((r, "order"))
            b.rs.append(op)
        for b in writes:
            if b.w is not None:
                deps.append((b.w, "order"))
            for r in self._collapse(b.rs):
                if r is not op:
                    deps.append((r, "order"))
            b.w = op
            b.rs = []

    def op(self, eng, fn, reads=(), writes=()):
        o = Op(eng, fn)
        self._add_deps(o, reads, writes)
        o.pos = len(self.streams[eng])
        self.streams[eng].append(o)
        return o

    def dma(self, eng, out, in_, key, reads=(), writes=(), **kw):
        def fn(e, out=out, in_=in_, kw=kw):
            return e.dma_start(out=out, in_=in_, **kw)
        o = Op(eng, fn, is_dma=True, dma_key=key)
        self._add_deps(o, reads, writes)
        if key in self.last_dma:
            o.deps.append((self.last_dma[key], "order"))
        self.last_dma[key] = o
        self.dma_counts[key] = self.dma_counts.get(key, 0) + 1
        o.count = 16 * self.dma_counts[key]
        o.pos = len(self.streams[eng])
        self.streams[eng].append(o)
        return o

    def finalize(self, final_waits=()):
        nc = self.nc
        for e in ENGS:
            for o in self.streams[e]:
                real = []
                for (p, kind) in o.deps:
                    if p.is_dma:
                        real.append(p)
                        continue
                    if (not o.is_dma) and p.eng == o.eng:
                        if e != PE:
                            real.append(p)
                        continue
                    real.append(p)
                o.deps = real
                for p in real:
                    if not p.is_dma:
                        p.needs_inc = True
        for e in ENGS:
            c = 0
            for o in self.streams[e]:
                if o.is_dma:
                    continue
                if o.needs_inc:
                    c += 1
                    o.count = c
        with contextlib.ExitStack() as es:
            esem = {e: es.enter_context(nc.semaphore("s_" + e)) for e in (PE, ACT, DVE, POOL)}
            dsem = {k: es.enter_context(nc.semaphore("d_%s" % (k,))) for k in self.dma_counts}
            block = es.enter_context(nc.Block())

            def emit(e, eng):
                waited = {}
                for o in self.streams[e]:
                    need = {}
                    for p in o.deps:
                        k = ("d", p.dma_key) if p.is_dma else ("e", p.eng)
                        s = dsem[p.dma_key] if p.is_dma else esem[p.eng]
                        if need.get(k, (None, 0))[1] < p.count:
                            need[k] = (s, p.count)
                    for k, (s, v) in need.items():
                        if waited.get(k, 0) >= v:
                            continue
                        waited[k] = v
                        eng.wait_ge(s, v)
                    ins = o.fn(eng)
                    if o.is_dma:
                        ins.then_inc(dsem[o.dma_key], 16)
                    elif o.needs_inc:
                        ins.then_inc(esem[e], 1)
                if e == SP:
                    for p in final_waits:
                        s = dsem[p.dma_key] if p.is_dma else esem[p.eng]
                        eng.wait_ge(s, p.count)

            @block.sync
            def _(eng):
                emit(SP, eng)

            @block.scalar
            def _(eng):
                emit(ACT, eng)

            @block.vector
            def _(eng):
                emit(DVE, eng)

            @block.gpsimd
            def _(eng):
                emit(POOL, eng)

            @block.tensor
            def _(eng):
                emit(PE, eng)


class TB:
    __slots__ = ("ap", "b")

    def __init__(self, ap, name, excl=False):
        self.ap = ap
        self.b = Buf(name, excl)


class Arena:
    def __init__(self, base_ap_f32, nwords):
        self.base = base_ap_f32
        self.nwords = nwords
        self.off = 0

    def reset(self):
        self.off = 0

    def get(self, shape, dt, name="a"):
        n = 1
        for s in shape:
            n *= s
        nbytes = n * (2 if dt == BF16 else 4)
        nw = (nbytes + 3) // 4
        nw = (nw + 7) // 8 * 8
        assert self.off + nw <= self.nwords, ("arena overflow", name, self.off, nw, self.nwords)
        ap = self.base[:, self.off:self.off + (nbytes + 3) // 4]
        self.off += nw
        if dt == BF16:
            ap = ap.bitcast(BF16)
        if len(shape) == 2:
            ap = ap.rearrange("p (a b) -> p a b", a=shape[0])
        elif len(shape) == 3:
            ap = ap.rearrange("p (a b c) -> p a b c", a=shape[0], b=shape[1])
        return ap


def build_program(kstop=None):
    nc = bass.Bass("TRN2", target_bir_lowering=False)
    p = Prog(nc)

    def din(name, shape):
        return nc.dram_tensor(name, list(shape), F32, kind="ExternalInput")

    x_d = din("x", [SEQ, D])
    ctx_d = din("ctx", [CTXL, D])
    cc_d = din("cc", [128, 8, 2])
    wada_d = din("w_ada", [2, D, 6 * D])
    bcol_d = din("b_col", [128, 2, 48])
    bada_d = din("b_ada", [2, 6 * D])
    win_d = din("w_in", [2, D, DIN])
    cw_d = din("cw", [128, 2, 2, 3])
    glng_d = din("gln_g", [2, 256])
    glnb_d = din("gln_b", [2, 256])
    wsT_d = din("wsT", [2, 128, 4, 128])
    bsT_d = din("bsT", [128, 2, 4])
    dec_d = din("dec", [128, 32])
    wout_d = din("w_out", [2, D, D])
    wff1_d = din("w_ff1", [2, D, DFF])
    wff2_d = din("w_ff2", [2, DFF, D])
    lng_d = din("ln_g", [2, 2, D])
    lnb_d = din("ln_b", [2, 2, D])
    y_d = nc.dram_tensor("y", [SEQ, D], F32, kind="ExternalOutput")
    x1_d = nc.dram_tensor("x1buf", [SEQ, D], F32)
    x2_d = nc.dram_tensor("x2buf", [SEQ, D], F32)
    xcm_d = nc.dram_tensor("xcm", [CTXL, D], F32)
    xc1_d = nc.dram_tensor("xc1", [CTXL, D], F32)
    hTl_d = nc.dram_tensor("hTl", [8, 128, 8 * 512], BF16)
    hTc_d = nc.dram_tensor("hTc", [1, 128, 8 * 256], BF16)

    def dram_tiles(t, n):
        return [TB(t[i * 128:(i + 1) * 128, :], "%s_%d" % (t.name, i)) for i in range(n)]

    xT = dram_tiles(x_d, 32)
    ctxT = dram_tiles(ctx_d, 2)
    yT_ = dram_tiles(y_d, 32)
    x1T = dram_tiles(x1_d, 32)
    x2T = dram_tiles(x2_d, 32)
    xcmT = dram_tiles(xcm_d, 2)
    xc1T = dram_tiles(xc1_d, 2)
    hTlG = [TB(hTl_d[g, :, :].rearrange("p (k n) -> p k n", k=8), "hTl%d" % g) for g in range(8)]
    hTcG = [TB(hTc_d[0, :, :].rearrange("p (k n) -> p k n", k=8), "hTc0")]

    def sbt(name, shape, dt=F32):
        t = nc.alloc_sbuf_tensor("sb_" + name, list(shape), dt)
        return t

    def full(t):
        nd = len(t.shape)
        return t[tuple(slice(None) for _ in range(nd))]

    banks = []
    for i in range(8):
        t = nc.alloc_psum_tensor("bank%d" % i, [128, 512], F32)
        banks.append(TB(t[:, :], "bank%d" % i, excl=True))
    bank_ctr = [0]

    def nb():
        b = banks[bank_ctr[0] % 8]
        bank_ctr[0] += 1
        return b

    NSLOT = 3
    wslots = []
    for i in range(NSLOT):
        t = sbt("wslot%d" % i, [128, 8192], BF16)
        wslots.append(TB(t[:, :], "wslot%d" % i))
    slot_ctr = [0]

    def load_w(src_ap, shape, cast=True):
        s = wslots[slot_ctr[0] % NSLOT]
        key = ("w%d" if cast else "wf%d") % (slot_ctr[0] % NSLOT)
        slot_ctr[0] += 1
        n = 1
        for v in shape:
            n *= v
        if cast:
            view = s.ap[:, 0:n]
        else:
            view = s.ap.bitcast(F32)[:, 0:n]
        if len(shape) == 2:
            view = view.rearrange("p (a b) -> p a b", a=shape[0])
        p.dma(POOL if cast else SP, view, src_ap, key=key, writes=[s.b])
        return view, s

    xl = []
    xr = []
    for i in range(2):
        t = sbt("xl%d" % i, [128, D])
        xl.append(TB(t[:, :], "xl%d" % i))
        t = sbt("xr%d" % i, [128, D])
        xr.append(TB(t[:, :], "xr%d" % i))
    pending_x = {}

    def prefetch_x(tiles, t_list):
        for t in t_list:
            if t < len(tiles):
                ld(xl[t % 2], tiles[t].ap, "xl%d" % (t % 2), src_tb=tiles[t])
                pending_x[id(tiles[t])] = t % 2

    rel = TB(full(sbt("rel", [128, 128])), "rel")
    ident = TB(full(sbt("ident", [128, 128])), "ident")
    identb = TB(full(sbt("identb", [128, 128], BF16)), "identb")
    iota1 = TB(full(sbt("iota1", [128, 128])), "iota1")
    iotar = TB(full(sbt("iotar", [128, 128])), "iotar")
    pcol = TB(full(sbt("pcol", [128, 1])), "pcol")
    pcolr = TB(full(sbt("pcolr", [128, 1])), "pcolr")
    lgall = TB(full(sbt("lgall", [128, 32])), "lgall")
    DT = TB(full(sbt("DT", [128, 8, 128], BF16)), "DT")
    dqf = TB(full(sbt("dqf", [128, 4, 128])), "dqf")
    dqb = TB(full(sbt("dqb", [128, 4, 128])), "dqb")
    dkf = TB(full(sbt("dkf", [128, 8])), "dkf")
    dkb = TB(full(sbt("dkb", [128, 8])), "dkb")
    lgself = TB(full(sbt("lgself", [128, 4])), "lgself")
    lgselb = TB(full(sbt("lgselb", [128, 4])), "lgselb")
    Gself = TB(full(sbt("Gself", [128, 4])), "Gself")
    Gselb = TB(full(sbt("Gselb", [128, 4])), "Gselb")
    Gtf = TB(full(sbt("Gtf", [128, 4, 64])), "Gtf")
    Gtb = TB(full(sbt("Gtb", [128, 4, 64])), "Gtb")
    tmp8 = TB(full(sbt("tmp8", [128, 8])), "tmp8")
    barr = TB(full(sbt("barr", [128, 1])), "barr")
    modcol = TB(full(sbt("modcol", [128, 48, 2])), "modcol")
    bcol = TB(full(sbt("bcol", [128, 2, 48])), "bcol")
    ccs = TB(full(sbt("ccs", [128, 8, 2])), "ccs")
    scs = TB(full(sbt("scs", [128, 8, 2])), "scs")
    scb = TB(full(sbt("scb", [128, 8, 2], BF16)), "scb")
    cwt = TB(full(sbt("cwt", [128, 2, 2, 3])), "cwt")
    glng = TB(full(sbt("glng", [128, 256])), "glng")
    glnb = TB(full(sbt("glnb", [128, 256])), "glnb")
    wsTb = TB(full(sbt("wsTb", [128, 4, 128], BF16)), "wsTb")
    bsT = TB(full(sbt("bsT", [128, 2, 4])), "bsT")
    SF = TB(full(sbt("SF", [128, 33, 4, 64], BF16)), "SF")
    SB = TB(full(sbt("SB", [128, 33, 4, 64], BF16)), "SB")
    SFc = TB(full(sbt("SFc", [128, 3, 4, 64], BF16)), "SFc")
    SBc = TB(full(sbt("SBc", [128, 3, 4, 64], BF16)), "SBc")
    cinf = TB(full(sbt("cinf", [128, 4, 64])), "cinf")
    cinb = TB(full(sbt("cinb", [128, 4, 64])), "cinb")
    zst = TB(full(sbt("zst", [128, 4, 64])), "zst")

    ARENA_W = 23040
    arena_t = sbt("arena", [128, ARENA_W])
    ar = Arena(arena_t[:, :], ARENA_W)
    phase_tok = [0]

    def V(fn, reads=(), writes=()):
        return p.op(DVE, fn, [r.b for r in reads], [w.b for w in writes])

    def A(fn, reads=(), writes=()):
        return p.op(ACT, fn, [r.b for r in reads], [w.b for w in writes])

    def G(fn, reads=(), writes=()):
        return p.op(POOL, fn, [r.b for r in reads], [w.b for w in writes])

    def T(fn, reads=(), writes=()):
        return p.op(PE, fn, [r.b for r in reads], [w.b for w in writes])

    def ld(dst, src_ap, key, src_tb=None):
        return p.dma(SP, dst.ap, src_ap, key=key, reads=[src_tb.b] if src_tb is not None else [], writes=[dst.b])

    arena_bufs = []

    last_barrier = [None]

    def AT(shape, dt, name):
        tb = TB(ar.get(shape, dt, name), name)
        tb.b.w = last_barrier[0]
        arena_bufs.append(tb)
        return tb

    stop = [False]
    phase_no = [0]

    def phase_switch():
        phase_no[0] += 1
        if kstop is not None and phase_no[0] > kstop:
            stop[0] = True
        if arena_bufs:
            last_barrier[0] = V(lambda e: e.memset(barr.ap, 0.0), writes=list(arena_bufs) + [barr])
        del arena_bufs[:]
        ar.reset()

    G(lambda e: e.iota(rel.ap, [[1, 128]], base=0, channel_multiplier=-1, allow_small_or_imprecise_dtypes=True),
      writes=[rel])
    G(lambda e: e.iota(iota1.ap, [[1, 128]], base=1, channel_multiplier=0, allow_small_or_imprecise_dtypes=True),
      writes=[iota1])
    G(lambda e: e.iota(iotar.ap, [[-1, 128]], base=128, channel_multiplier=0, allow_small_or_imprecise_dtypes=True),
      writes=[iotar])
    G(lambda e: e.iota(pcol.ap, [[0, 1]], base=0, channel_multiplier=1, allow_small_or_imprecise_dtypes=True),
      writes=[pcol])
    G(lambda e: e.iota(pcolr.ap, [[0, 1]], base=127, channel_multiplier=-1, allow_small_or_imprecise_dtypes=True),
      writes=[pcolr])
    V(lambda e: e.tensor_single_scalar(ident.ap, rel.ap, 0.0, op=ALU.is_equal), [rel], [ident])
    V(lambda e: e.tensor_copy(identb.ap, ident.ap), [ident], [identb])
    V(lambda e: e.memset(zst.ap, 0.0), writes=[zst])
    V(lambda e: e.memset(SFc.ap, 0.0), writes=[SFc])
    V(lambda e: e.memset(SBc.ap, 0.0), writes=[SBc])
    ld(lgall, dec_d[:, :], "misc")
    ld(bcol, bcol_d[:, :, :], "misc")
    ld(ccs, cc_d[:, :, :], "misc")
    ld(cwt, cw_d[:, :, :, :], "misc")
    ld(bsT, bsT_d[:, :, :], "misc")
    A(lambda e: e.activation(lgall.ap, lgall.ap, AF.Exp), [lgall], [lgall])
    V(lambda e: e.tensor_scalar_mul(lgall.ap, lgall.ap, -1.0), [lgall], [lgall])
    A(lambda e: e.activation(scs.ap, ccs.ap, AF.Silu), [ccs], [scs])
    V(lambda e: e.tensor_copy(scb.ap, scs.ap), [scs], [scb])

    def layer_tables(l):
        phase_switch()
        if stop[0]:
            return
        relpos = AT([128], F32, "relpos")
        relneg = AT([128], F32, "relneg")
        maskf = AT([128], F32, "maskf")
        maskb = AT([128], F32, "maskb")
        tmpA = AT([128], F32, "tmpA")
        tmpB = AT([128], F32, "tmpB")
        V(lambda e: e.tensor_single_scalar(maskf.ap, rel.ap, 0.0, op=ALU.is_ge), [rel], [maskf])
        V(lambda e: e.tensor_single_scalar(maskb.ap, rel.ap, 0.0, op=ALU.is_lt), [rel], [maskb])
        V(lambda e: e.tensor_scalar_max(relpos.ap, rel.ap, 0.0), [rel], [relpos])
        V(lambda e: e.tensor_sub(relneg.ap, relpos.ap, rel.ap), [relpos, rel], [relneg])
        lgf = lgall.ap[:, l * 8:(l + 1) * 8]
        lgb = lgall.ap[:, 16 + l * 8:16 + (l + 1) * 8]
        for h in range(8):
            A(lambda e, h=h: e.activation(tmpA.ap, relpos.ap, AF.Exp, scale=lgf[:, h:h + 1]), [relpos, lgall], [tmpA])
            A(lambda e, h=h: e.activation(tmpB.ap, relneg.ap, AF.Exp, scale=lgb[:, h:h + 1]), [relneg, lgall], [tmpB])
            V(lambda e: e.tensor_mul(tmpA.ap, tmpA.ap, maskf.ap), [tmpA, maskf], [tmpA])
            V(lambda e: e.tensor_mul(tmpB.ap, tmpB.ap, maskb.ap), [tmpB, maskb], [tmpB])
            V(lambda e, h=h: e.tensor_add(DT.ap[:, h, :], tmpA.ap, tmpB.ap), [tmpA, tmpB], [DT])
        for (sel, lg) in ((lgself, lgf), (lgselb, lgb)):
            V(lambda e, sel=sel, lg=lg: e.tensor_copy(sel.ap[0:64, :], lg[0:64, 0::2]), [lgall], [sel])
            V(lambda e, sel=sel, lg=lg: e.tensor_copy(sel.ap[64:128, :], lg[64:128, 1::2]), [lgall], [sel])
        for pr in range(4):
            A(lambda e, pr=pr: e.activation(tmpA.ap, iota1.ap, AF.Exp, scale=lgself.ap[:, pr:pr + 1]),
              [iota1, lgself], [tmpA])
            V(lambda e, pr=pr: e.tensor_scalar_mul(dqf.ap[:, pr, :], tmpA.ap, 0.125), [tmpA], [dqf])
            A(lambda e, pr=pr: e.activation(tmpB.ap, iotar.ap, AF.Exp, scale=lgselb.ap[:, pr:pr + 1]),
              [iotar, lgselb], [tmpB])
            V(lambda e, pr=pr: e.tensor_scalar_mul(dqb.ap[:, pr, :], tmpB.ap, 0.125), [tmpB], [dqb])
        V(lambda e: e.tensor_scalar_mul(tmp8.ap, lgf, pcolr.ap[:, 0:1]), [lgall, pcolr], [tmp8])
        A(lambda e: e.activation(dkf.ap, tmp8.ap, AF.Exp), [tmp8], [dkf])
        V(lambda e: e.tensor_scalar_mul(tmp8.ap, lgb, pcol.ap[:, 0:1]), [lgall, pcol], [tmp8])
        A(lambda e: e.activation(dkb.ap, tmp8.ap, AF.Exp), [tmp8], [dkb])
        A(lambda e: e.activation(Gself.ap, lgself.ap, AF.Exp, scale=128.0), [lgself], [Gself])
        A(lambda e: e.activation(Gselb.ap, lgselb.ap, AF.Exp, scale=128.0), [lgselb], [Gselb])
        V(lambda e: e.tensor_copy(Gtf.ap, Gself.ap.unsqueeze(2).to_broadcast([128, 4, 64])), [Gself], [Gtf])
        V(lambda e: e.tensor_copy(Gtb.ap, Gselb.ap.unsqueeze(2).to_broadcast([128, 4, 64])), [Gselb], [Gtb])
        ld(glng, glng_d[l:l + 1, :].to_broadcast([128, 256]), "misc")
        ld(glnb, glnb_d[l:l + 1, :].to_broadcast([128, 256]), "misc")
        wsTf = AT([4, 128], F32, "wsTf")
        ld(wsTf, wsT_d[l, :, :, :], "misc")
        V(lambda e: e.tensor_copy(wsTb.ap, wsTf.ap), [wsTf], [wsTb])

    def modulation(l):
        bk = nb()
        wv_all = wada_d[l, :, :].rearrange("(k p) n -> p k n", p=128)
        for pc in range(12):
            wv, ws = load_w(wv_all[:, :, pc * 512:(pc + 1) * 512], [8, 512], cast=True)
            for jj in range(4):
                j = pc * 4 + jj
                for k in range(8):
                    T(lambda e, j=j, jj=jj, k=k, wv=wv: e.matmul(
                        bk.ap[:, 2 * j:2 * j + 2], wv[:, k, jj * 128:(jj + 1) * 128], scb.ap[:, k, :],
                        start=(k == 0), stop=(k == 7)), [ws, scb], [bk])
        V(lambda e: e.tensor_tensor(modcol.ap, bk.ap[:, 0:96].rearrange("p (a b) -> p a b", b=2),
                                    bcol.ap[:, l, :].unsqueeze(2).to_broadcast([128, 48, 2]), op=ALU.add),
          [bk, bcol], [modcol])
        V(lambda e: e.tensor_scalar_add(modcol.ap[:, 8:16, :], modcol.ap[:, 8:16, :], 1.0), [modcol], [modcol])
        V(lambda e: e.tensor_scalar_add(modcol.ap[:, 32:40, :], modcol.ap[:, 32:40, :], 1.0), [modcol], [modcol])

    def gate_table(l, m, which, dst, screp, brow):
        V(lambda e: e.tensor_copy(screp.ap, scs.ap[:, :, which:which + 1].to_broadcast([128, 8, 128])),
          [scs], [screp])
        ld(brow, bada_d[l:l + 1, m * 1024:(m + 1) * 1024].to_broadcast([128, 1024]), "brow")
        wv_all = wada_d[l, :, :].rearrange("(k p) n -> p k n", p=128)
        for hf in range(2):
            c0 = m * 1024 + hf * 512
            wv, ws = load_w(wv_all[:, :, c0:c0 + 512], [8, 512], cast=True)
            bk = nb()
            for k in range(8):
                T(lambda e, k=k, wv=wv, bk=bk: e.matmul(bk.ap, screp.ap[:, k, :], wv[:, k, :],
                                                          start=(k == 0), stop=(k == 7)), [ws, screp], [bk])
            V(lambda e, hf=hf, bk=bk: e.tensor_add(dst.ap[:, hf * 512:(hf + 1) * 512], bk.ap,
                                                    brow.ap[:, hf * 512:(hf + 1) * 512]), [bk, brow], [dst])

    def gelu(bk, dst, tmp=None):
        A(lambda e: e.activation(dst.ap, bk.ap[:, 0:256], AF.Gelu_apprx_tanh), [bk], [dst])

    def row_table(dst, src_row_ap, key):
        ld(dst, src_row_ap.to_broadcast([128, 1024]), key)

    def make_hT(tiles_src, nt, which, m_shift, m_scale, hT, hTb, next_tiles=None, only=None):
        for t in (range(nt) if only is None else only):
            xs = xl[t % 2]
            if pending_x.pop(id(tiles_src[t]), None) is None:
                ld(xs, tiles_src[t].ap, "xl%d" % (t % 2), src_tb=tiles_src[t])
            for kg in range(2):
                bk = nb()
                for kk in range(4):
                    k = kg * 4 + kk
                    T(lambda e, k=k, kk=kk, bk=bk, xs=xs: e.transpose(
                        bk.ap[:, kk * 128:(kk + 1) * 128], xs.ap[:, k * 128:(k + 1) * 128], ident.ap),
                      [xs, ident], [bk])
                for kk in range(4):
                    k = kg * 4 + kk
                    sc = modcol.ap[:, m_scale * 8 + k, which:which + 1]
                    sh = modcol.ap[:, m_shift * 8 + k, which:which + 1]
                    o_ap = hT.ap[:, k, t * 128:(t + 1) * 128]
                    i_ap = bk.ap[:, kk * 128:(kk + 1) * 128]
                    if kk % 2 == 0:
                        A(lambda e, o_ap=o_ap, i_ap=i_ap, sc=sc, sh=sh: e.activation(
                            o_ap, i_ap, AF.Identity, bias=sh, scale=sc), [bk, modcol], [hTb[t][kg]])
                    else:
                        V(lambda e, o_ap=o_ap, i_ap=i_ap, sc=sc, sh=sh: e.tensor_scalar(
                            o_ap, i_ap, sc, sh, op0=ALU.mult, op1=ALU.add), [bk, modcol], [hTb[t][kg]])
        if next_tiles:
            prefetch_x(next_tiles, [0, 1])

    def hT_reads(hTb, tiles, k):
        return [hTb[t][k // 4] for t in tiles]

    class BB:
        def __init__(self, b):
            self.b = b

    def mk_hTb(nt, name):
        r = [[BB(Buf("%s_%d_%d" % (name, t, g))) for g in range(2)] for t in range(nt)]
        for row in r:
            for bb in row:
                bb.b.w = last_barrier[0]
                arena_bufs.append(bb)
        return r

    def pass1(l, tiles_src, nt_total, which, init_f, init_b, SFo, SBo, want_own, hTG):
        phase_switch()
        if stop[0]:
            return zst, zst
        ng = nt_total // min(4, nt_total)
        gsz = min(4, nt_total)
        hT = AT([8, gsz * 128], BF16, "p1_hT")
        dSb = AT([nt_total, 4, 64], F32, "p1_dSb")
        kdf = [AT([512], BF16, "p1_kdf%d" % i) for i in range(2)]
        kdb = [AT([512], BF16, "p1_kdb%d" % i) for i in range(2)]
        vbf = [AT([512], BF16, "p1_v%d" % i) for i in range(2)]
        stf = [AT([4, 64], F32, "p1_stf%d" % i) for i in range(2)]
        stb = [AT([4, 64], F32, "p1_stb%d" % i) for i in range(2)]
        wall = win_d[l, :, :].rearrange("(k p) n -> p k n", p=128)
        wk, wks = load_w(wall[:, :, 1792:2304], [8, 512])
        wvv, wvs = load_w(wall[:, :, 2304:2816], [8, 512])
        cur = 0
        V(lambda e: e.tensor_copy(stf[0].ap, init_f.ap), [init_f], [stf[0]])
        if want_own:
            A(lambda e: e.activation(SFo.ap[:, 0, :, :], stf[0].ap, AF.Copy), [stf[0]], [SFo])
        hTb = mk_hTb(gsz, "p1hT")
        for g in range(ng):
            make_hT(tiles_src[g * gsz:(g + 1) * gsz], gsz, which, 0, 1, hT, hTb,
                    next_tiles=tiles_src[(g + 1) * gsz:(g + 2) * gsz] if g + 1 < ng else None)
            p.dma(POOL, hTG[g].ap, hT.ap, key="hTst", reads=[bb.b for row in hTb for bb in row], writes=[hTG[g].b])
            for t in range(gsz):
                c = g * gsz + t
                i2 = c % 2
                bk_k = nb()
                bk_v = nb()
                for k in range(8):
                    T(lambda e, k=k, t=t, bk_k=bk_k: e.matmul(bk_k.ap, hT.ap[:, k, t * 128:(t + 1) * 128], wk[:, k, :],
                                                                start=(k == 0), stop=(k == 7)),
                      [hTb[t][k // 4], wks], [bk_k])
                for k in range(8):
                    T(lambda e, k=k, t=t, bk_v=bk_v: e.matmul(bk_v.ap, hT.ap[:, k, t * 128:(t + 1) * 128], wvv[:, k, :],
                                                                start=(k == 0), stop=(k == 7)),
                      [hTb[t][k // 4], wvs], [bk_v])
                kview = bk_k.ap.rearrange("p (h d) -> p h d", h=8)
                V(lambda e, i2=i2, kview=kview: e.tensor_tensor(
                    kdf[i2].ap.rearrange("p (h d) -> p h d", h=8), kview,
                    dkf.ap.unsqueeze(2).to_broadcast([128, 8, 64]), op=ALU.mult), [bk_k, dkf], [kdf[i2]])
                V(lambda e, i2=i2, kview=kview: e.tensor_tensor(
                    kdb[i2].ap.rearrange("p (h d) -> p h d", h=8), kview,
                    dkb.ap.unsqueeze(2).to_broadcast([128, 8, 64]), op=ALU.mult), [bk_k, dkb], [kdb[i2]])
                A(lambda e, i2=i2, bk_v=bk_v: e.activation(vbf[i2].ap, bk_v.ap, AF.Copy), [bk_v], [vbf[i2]])
                bf_ = nb()
                bb_ = nb()
                for pr in range(4):
                    T(lambda e, pr=pr, i2=i2, bf_=bf_: e.matmul(
                        bf_.ap[:, pr * 128:(pr + 1) * 128], kdf[i2].ap[:, pr * 128:(pr + 1) * 128],
                        vbf[i2].ap[:, pr * 128:(pr + 1) * 128], start=True, stop=True), [kdf[i2], vbf[i2]], [bf_])
                for pr in range(4):
                    T(lambda e, pr=pr, i2=i2, bb_=bb_: e.matmul(
                        bb_.ap[:, pr * 128:(pr + 1) * 128], kdb[i2].ap[:, pr * 128:(pr + 1) * 128],
                        vbf[i2].ap[:, pr * 128:(pr + 1) * 128], start=True, stop=True), [kdb[i2], vbf[i2]], [bb_])
                nxt = 1 - cur
                V(lambda e, cur=cur, nxt=nxt: e.tensor_mul(stf[nxt].ap, stf[cur].ap, Gtf.ap), [stf[cur], Gtf], [stf[nxt]])
                for hl in range(2):
                    ps_ = slice(hl * 64, hl * 64 + 64)
                    dsv = bf_.ap.rearrange("p (a b) -> p a b", a=4)[ps_, :, hl * 64:hl * 64 + 64]
                    V(lambda e, nxt=nxt, ps_=ps_, dsv=dsv: e.tensor_add(stf[nxt].ap[ps_, :, :], stf[nxt].ap[ps_, :, :], dsv),
                      [stf[nxt], bf_], [stf[nxt]])
                    dsv2 = bb_.ap.rearrange("p (a b) -> p a b", a=4)[ps_, :, hl * 64:hl * 64 + 64]
                    A(lambda e, c=c, ps_=ps_, dsv2=dsv2: e.activation(dSb.ap[ps_, c, :, :], dsv2, AF.Copy), [bb_], [dSb])
                cur = nxt
                if want_own or True:
                    if c + 1 < nt_total or True:
                        A(lambda e, c=c, cur=cur: e.activation(SFo.ap[:, c + 1, :, :], stf[cur].ap, AF.Copy),
                          [stf[cur]], [SFo])
        fin_f = stf[cur]
        curb = 0
        V(lambda e: e.tensor_copy(stb[0].ap, init_b.ap), [init_b], [stb[0]])
        A(lambda e: e.activation(SBo.ap[:, nt_total - 1, :, :], stb[0].ap, AF.Copy), [stb[0]], [SBo])
        for c in range(nt_total - 1, -1, -1):
            nxt = 1 - curb
            V(lambda e, curb=curb, nxt=nxt: e.tensor_mul(stb[nxt].ap, stb[curb].ap, Gtb.ap), [stb[curb], Gtb], [stb[nxt]])
            V(lambda e, nxt=nxt, c=c: e.tensor_add(stb[nxt].ap, stb[nxt].ap, dSb.ap[:, c, :, :]), [stb[nxt], dSb], [stb[nxt]])
            curb = nxt
            idx = c - 1 if c > 0 else nt_total
            A(lambda e, idx=idx, curb=curb: e.activation(SBo.ap[:, idx, :, :], stb[curb].ap, AF.Copy), [stb[curb]], [SBo])
        fin_b = stb[curb]
        return fin_f, fin_b

    def layer_norm_store(l, which_ln, bkA, bkB, xres, gt, lng_t, lnb_t, tbuf, stat, dst_tile, key, store_eng=None):
        for hf, bk in enumerate((bkA, bkB)):
            sl = slice(hf * 512, (hf + 1) * 512)
            V(lambda e, bk=bk, sl=sl: e.tensor_mul(tbuf.ap[:, sl], bk.ap, gt.ap[:, sl]), [bk, gt], [tbuf])
        ln_tail(xres, tbuf, lng_t, lnb_t, stat, dst_tile, key, store_eng=store_eng)

    def ln_tail(xres, tbuf, lng_t, lnb_t, stat, dst_tile, key, aff=None, store_eng=None):
        V(lambda e: e.scalar_tensor_tensor(tbuf.ap, xres.ap, ALPHA, tbuf.ap, op0=ALU.mult, op1=ALU.add),
          [xres, tbuf], [tbuf])
        for hf in range(2):
            V(lambda e, hf=hf: e.bn_stats(stat.ap[:, hf * 6:(hf + 1) * 6], tbuf.ap[:, hf * 512:(hf + 1) * 512]),
              [tbuf], [stat])
        V(lambda e: e.bn_aggr(stat.ap[:, 12:14], stat.ap[:, 0:12]), [stat], [stat])
        A(lambda e: e.activation(stat.ap[:, 14:15], stat.ap[:, 13:14], AF.Sqrt, bias=epsc.ap[:, 0:1]), [stat, epsc], [stat])
        V(lambda e: e.reciprocal(stat.ap[:, 15:16], stat.ap[:, 14:15]), [stat], [stat])
        V(lambda e: e.scalar_tensor_tensor(stat.ap[:, 16:17], stat.ap[:, 12:13], -1.0, stat.ap[:, 15:16],
                                           op0=ALU.mult, op1=ALU.mult), [stat], [stat])
        A(lambda e: e.activation(tbuf.ap, tbuf.ap, AF.Identity, bias=stat.ap[:, 16:17], scale=stat.ap[:, 15:16]),
          [tbuf, stat], [tbuf])
        aff = aff or V
        aff(lambda e: e.tensor_mul(tbuf.ap, tbuf.ap, lng_t.ap), [tbuf, lng_t], [tbuf])
        aff(lambda e: e.tensor_add(tbuf.ap, tbuf.ap, lnb_t.ap), [tbuf, lnb_t], [tbuf])
        return p.dma(store_eng or ACT, dst_tile.ap, tbuf.ap, key=key, reads=[tbuf.b], writes=[dst_tile.b])

    epsc = TB(full(sbt("epsc", [128, 1])), "epsc")
    V(lambda e: e.memset(epsc.ap, EPS), writes=[epsc])

    def mixer(l, tiles_src, tiles_dst, nt_total, which, SFo, SBo, rowlen, hTG):
        phase_switch()
        if stop[0]:
            return
        gsz = min(NTM, nt_total)
        ng = nt_total // gsz
        NTOK = gsz * 128
        gt = AT([1024], F32, "m_gt")
        lng_t = AT([1024], F32, "m_lng")
        lnb_t = AT([1024], F32, "m_lnb")
        hT = AT([8, NTOK], BF16, "m_hT")
        qT = AT([4, NTOK], BF16, "m_qT")
        qdfT = AT([4, NTOK], BF16, "m_qdf")
        qdbT = AT([4, NTOK], BF16, "m_qdb")
        kT = AT([4, NTOK], BF16, "m_kT")
        yT = AT([8, NTOK], BF16, "m_yT")
        xin = [AT([max(NTOK, 512)], F32, "m_xin%d" % i) for i in range(2)]
        zc = [AT([max(NTOK, 512)], F32, "m_z%d" % i) for i in range(2)]
        yc = xin
        gtmp = None
        ug = [AT([256], F32, "m_ug%d" % i) for i in range(gsz)]
        vgl = [AT([256], F32, "m_vg%d" % i) for i in range(2)]
        vln = [AT([256], BF16, "m_vln%d" % i) for i in range(gsz)]
        vrb = [AT([512], BF16, "m_vr%d" % i) for i in range(gsz)]
        sg = [AT([512], F32, "m_sg%d" % i) for i in range(gsz)]
        PT = [[AT([4, 128], BF16, "m_PT%d_%d" % (i, j)) for j in range(2)] for i in range(2)]
        ygm = [AT([256], F32, "m_ygm%d" % i) for i in range(2)]
        tb = [AT([1024], F32, "m_tb%d" % i) for i in range(2)]
        stat = [AT([24], F32, "m_stat%d" % i) for i in range(2)]
        gstl = [AT([48], F32, "m_gst%d" % i) for i in range(4)]
        gst = gstl[0]
        osbl, osql = [], []
        for i in range(2):
            o_ = TB(xin[i].ap[:, 0:512], "m_osb%d" % i)
            o_.b = xin[i].b
            q_ = TB(zc[i].ap[:, 0:512], "m_osq%d" % i)
            q_.b = zc[i].b
            osbl.append(o_)
            osql.append(q_)
        screp = TB(tb[1].ap.bitcast(BF16)[:, 0:1024].rearrange("p (a b) -> p a b", a=8), "m_screp")
        screp.b = tb[1].b
        gate_table(l, 2, which, gt, screp, tb[0])
        row_table(lng_t, lng_d[l, 0:1, :], "lng")
        row_table(lnb_t, lnb_d[l, 0:1, :], "lnb")
        wall = win_d[l, :, :].rearrange("(k p) n -> p k n", p=128)
        woall = wout_d[l, :, :].rearrange("(k p) n -> p k n", p=128)
        nrow = NTOK // rowlen

        hTb = mk_hTb(gsz, "mhT")
        w0_pre = [(load_w(wall[:, :, 0:1024], [8, 1024]), load_w(wall[:, :, 1024:2048], [8, 1024]))]
        for g in range(ng):
            if g == 0:
                p.dma(SP, hT.ap, hTG[0].ap, key="hTld", reads=[hTG[0].b], writes=[bb.b for row in hTb for bb in row])
            alltiles = list(range(gsz))

            def fm(wv, ws, col, bk):
                for k in range(8):
                    T(lambda e, k=k, wv=wv, col=col, bk=bk: e.matmul(
                        bk.ap[:, 0:NTOK], wv[:, k, col:col + 128], hT.ap[:, k, :], start=(k == 0), stop=(k == 7)),
                      [ws] + hT_reads(hTb, alltiles, k), [bk])

            def tm(wv, ws, col, n, t, bk, off):
                for k in range(8):
                    T(lambda e, k=k, wv=wv, col=col, bk=bk, t=t, n=n, off=off: e.matmul(
                        bk.ap[:, off:off + n], hT.ap[:, k, t * 128:(t + 1) * 128], wv[:, k, col:col + n],
                        start=(k == 0), stop=(k == 7)), [ws, hTb[t][k // 4]], [bk])

            MS = int(os.environ.get("MSTOP", "99"))
            if MS < 1:
                return
            (w0, s0), (w1, s1) = w0_pre[0]
            for c2 in range(2):
                bx = nb()
                fm(w0, s0, c2 * 128, bx)
                A(lambda e, c2=c2, bx=bx: e.activation(xin[c2].ap[:, 0:NTOK], bx.ap[:, 0:NTOK], AF.Copy), [bx], [xin[c2]])
                bc = nb()
                fm(w0, s0, 512 + c2 * 128, bc)
                V(lambda e, c2=c2, bc=bc: e.tensor_mul(zc[c2].ap[:, 0:NTOK], bc.ap[:, 0:NTOK], xin[c2].ap[:, 0:NTOK]), [bc, xin[c2]], [zc[c2]])
                cwv = cwt.ap[:, l, c2, :]
                z3 = zc[c2].ap[:, 0:NTOK].rearrange("p (r w) -> p r w", w=rowlen)
                y3 = yc[c2].ap[:, 0:NTOK].rearrange("p (r w) -> p r w", w=rowlen)
                A(lambda e, c2=c2, cwv=cwv: e.activation(yc[c2].ap[:, 0:NTOK], zc[c2].ap[:, 0:NTOK], AF.Identity, scale=cwv[:, 1:2]),
                  [zc[c2], cwt], [yc[c2]])
                V(lambda e, c2=c2, cwv=cwv, z3=z3, y3=y3: e.scalar_tensor_tensor(
                    y3[:, :, 1:rowlen], z3[:, :, 0:rowlen - 1], cwv[:, 0:1], y3[:, :, 1:rowlen],
                    op0=ALU.mult, op1=ALU.add), [zc[c2], cwt, yc[c2]], [yc[c2]])
                V(lambda e, c2=c2, cwv=cwv, z3=z3, y3=y3: e.scalar_tensor_tensor(
                    y3[:, :, 0:rowlen - 1], z3[:, :, 1:rowlen], cwv[:, 2:3], y3[:, :, 0:rowlen - 1],
                    op0=ALU.mult, op1=ALU.add), [zc[c2], cwt, yc[c2]], [yc[c2]])
                bg = nb()
                fm(w0, s0, 256 + c2 * 128, bg)
                V(lambda e, c2=c2, bg=bg: e.tensor_mul(yT.ap[:, c2, :], bg.ap[:, 0:NTOK], yc[c2].ap[:, 0:NTOK]), [bg, yc[c2]], [yT])
            for t in range(gsz):
                bu = nb()
                tm(w0, s0, 768, 256, t, bu, 0)
                gelu(bu, ug[t], gtmp)
            if MS < 2:
                return
            for t in range(gsz):
                bv = nb()
                tm(w1, s1, 0, 256, t, bv, 0)
                vg = vgl[t % 2]
                gelu(bv, vg, gtmp)
                gs = gstl[t % 4]
                V(lambda e, vg=vg, gs=gs: e.bn_stats(gs.ap[:, 32:38], vg.ap), [vg], [gs])
                V(lambda e, gs=gs: e.bn_aggr(gs.ap[:, 38:40], gs.ap[:, 32:38]), [gs], [gs])
                A(lambda e, gs=gs: e.activation(gs.ap[:, 40:41], gs.ap[:, 39:40], AF.Sqrt, bias=epsc.ap[:, 0:1]), [gs, epsc], [gs])
                V(lambda e, gs=gs: e.reciprocal(gs.ap[:, 41:42], gs.ap[:, 40:41]), [gs], [gs])
                V(lambda e, vg=vg, gs=gs: e.tensor_scalar(vg.ap, vg.ap, gs.ap[:, 38:39], gs.ap[:, 41:42],
                                                          op0=ALU.subtract, op1=ALU.mult), [vg, gs], [vg])
                V(lambda e, vg=vg: e.tensor_mul(vg.ap, vg.ap, glng.ap), [vg, glng], [vg])
                V(lambda e, t=t, vg=vg: e.tensor_add(vln[t].ap, vg.ap, glnb.ap), [vg, glnb], [vln[t]])
            for c4 in range(4):
                bq = nb()
                fm(w1, s1, 256 + c4 * 128, bq)
                A(lambda e, c4=c4, bq=bq: e.activation(qT.ap[:, c4, :], bq.ap[:, 0:NTOK], AF.Copy, scale=0.125), [bq], [qT])
                V(lambda e, c4=c4, bq=bq: e.tensor_tensor(
                    qdfT.ap[:, c4, :].rearrange("p (t i) -> p t i", i=128), bq.ap[:, 0:NTOK].rearrange("p (t i) -> p t i", i=128),
                    dqf.ap[:, c4, :].unsqueeze(1).to_broadcast([128, gsz, 128]), op=ALU.mult), [bq, dqf], [qdfT])
                V(lambda e, c4=c4, bq=bq: e.tensor_tensor(
                    qdbT.ap[:, c4, :].rearrange("p (t i) -> p t i", i=128), bq.ap[:, 0:NTOK].rearrange("p (t i) -> p t i", i=128),
                    dqb.ap[:, c4, :].unsqueeze(1).to_broadcast([128, gsz, 128]), op=ALU.mult), [bq, dqb], [qdbT])
            for c4 in range(2):
                bkk = nb()
                fm(w1, s1, 768 + c4 * 128, bkk)
                A(lambda e, c4=c4, bkk=bkk: e.activation(kT.ap[:, c4, :], bkk.ap[:, 0:NTOK], AF.Copy), [bkk], [kT])
            if MS < 3:
                return
            w2, s2 = load_w(wall[:, :, 2048:3072], [8, 1024])
            for c4 in range(2, 4):
                bkk = nb()
                fm(w2, s2, (c4 - 2) * 128, bkk)
                A(lambda e, c4=c4, bkk=bkk: e.activation(kT.ap[:, c4, :], bkk.ap[:, 0:NTOK], AF.Copy), [bkk], [kT])
            for t in range(gsz):
                bvr = nb()
                tm(w2, s2, 256, 512, t, bvr, 0)
                A(lambda e, t=t, bvr=bvr: e.activation(vrb[t].ap, bvr.ap, AF.Copy), [bvr], [vrb[t]])
            w3, s3 = load_w(wall[:, :, 3072:3328], [8, 256])
            for t in range(gsz):
                bgt = nb()
                tm(w2, s2, 768, 256, t, bgt, 0)
                tm(w3, s3, 0, 256, t, bgt, 256)
                A(lambda e, t=t, bgt=bgt: e.activation(sg[t].ap, bgt.ap, AF.Silu), [bgt], [sg[t]])
            if MS < 4:
                return
            wo, so = load_w(woall[:, :, :], [8, 1024])
            if g + 1 < ng:
                w0_pre[0] = (load_w(wall[:, :, 0:1024], [8, 1024]), load_w(wall[:, :, 1024:2048], [8, 1024]))
            if g + 1 < ng:
                p.dma(SP, hT.ap, hTG[g + 1].ap, key="hTld", reads=[hTG[g + 1].b],
                      writes=[bb.b for row in hTb for bb in row])
            xrm = [xr[0], xr[1], xl[0], xl[1]]
            for t in range(gsz):
                ld(xrm[t], tiles_src[g * gsz + t].ap, "mxr%d" % t, src_tb=tiles_src[g * gsz + t])
            st = [dict() for _ in range(gsz)]

            def stA(t):
                c = g * gsz + t
                tsl = slice(t * 128, (t + 1) * 128)
                d = st[t]
                bm = nb()
                for h in range(4):
                    T(lambda e, h=h, t=t, bm=bm: e.matmul(bm.ap[:, h * 64:(h + 1) * 64], wsTb.ap[:, h, :],
                                                           vln[t].ap[:, h * 64:(h + 1) * 64], start=True, stop=True),
                      [wsTb, vln[t]], [bm])
                bsA, bsB = nb(), nb()
                for h in range(8):
                    pr, hl = h // 2, h % 2
                    ps_ = slice(hl * 64, hl * 64 + 64)
                    bs_ = bsA if hl == 0 else bsB
                    T(lambda e, pr=pr, ps_=ps_, tsl=tsl, bs_=bs_: e.matmul(
                        bs_.ap[:, pr * 128:(pr + 1) * 128], kT.ap[ps_, pr, tsl], qT.ap[ps_, pr, tsl],
                        start=True, stop=True), [kT, qT], [bs_])
                yg = ygm[t % 2]
                for h in range(4):
                    V(lambda e, h=h, t=t, bm=bm, yg=yg: e.scalar_tensor_tensor(
                        yg.ap[:, h * 64:(h + 1) * 64], bm.ap[:, h * 64:(h + 1) * 64], bsT.ap[:, l, h:h + 1],
                        ug[t].ap[:, h * 64:(h + 1) * 64], op0=ALU.add, op1=ALU.mult), [bm, bsT, ug[t]], [yg])
                for hl, bs_ in enumerate((bsA, bsB)):
                    pt = PT[t % 2][hl]
                    V(lambda e, hl=hl, bs_=bs_, pt=pt: e.tensor_tensor(
                        pt.ap[:, 0:4, :], bs_.ap.rearrange("p (a b) -> p a b", a=4), DT.ap[:, hl::2, :],
                        op=ALU.mult), [bs_, DT], [pt])

            def stB(t):
                c = g * gsz + t
                tsl = slice(t * 128, (t + 1) * 128)
                yg = ygm[t % 2]
                btr = nb()
                btv = btr.ap
                for j in range(2):
                    T(lambda e, j=j, btv=btv, yg=yg: e.transpose(btv[:, j * 128:(j + 1) * 128], yg.ap[:, j * 128:(j + 1) * 128],
                                                                  ident.ap), [yg, ident], [btr])
                bo = nb()
                st[t]["bo"] = bo
                for h in range(8):
                    pr, hl = h // 2, h % 2
                    ps_ = slice(hl * 64, hl * 64 + 64)
                    osl = slice(h * 64, (h + 1) * 64)
                    pt = PT[t % 2][hl]
                    T(lambda e, pr=pr, t=t, osl=osl, pt=pt, bo=bo: e.matmul(
                        bo.ap[:, osl], pt.ap[:, pr, :], vrb[t].ap[:, osl], start=True, stop=False), [pt, vrb[t]], [bo])
                    T(lambda e, pr=pr, ps_=ps_, tsl=tsl, osl=osl, c=c, bo=bo: e.matmul(
                        bo.ap[:, osl], qdfT.ap[ps_, pr, tsl], SFo.ap[ps_, c, pr, :], start=False, stop=False),
                      [qdfT, SFo], [bo])
                    T(lambda e, pr=pr, ps_=ps_, tsl=tsl, osl=osl, c=c, bo=bo: e.matmul(
                        bo.ap[:, osl], qdbT.ap[ps_, pr, tsl], SBo.ap[ps_, c, pr, :], start=False, stop=True),
                      [qdbT, SBo], [bo])
                A(lambda e, tsl=tsl, btv=btv: e.activation(
                    yT.ap[:, 2:4, tsl], btv[:, 0:256].rearrange("p (a b) -> p a b", a=2), AF.Copy), [btr], [yT])
                osb, osq = osbl[t % 2], osql[t % 2]
                A(lambda e, bo=bo, osb=osb: e.activation(osb.ap, bo.ap, AF.Copy), [bo], [osb])
                A(lambda e, bo=bo, osq=osq: e.activation(osq.ap, bo.ap, AF.Square), [bo], [osq])

            def stC(t):
                osb, osq = osbl[t % 2], osql[t % 2]
                gs = gstl[t % 4]
                o3 = osb.ap.rearrange("p (h e) -> p h e", h=8)
                q3 = osq.ap.rearrange("p (h e) -> p h e", h=8)
                V(lambda e: e.tensor_reduce(gs.ap[:, 0:8], o3, axis=AX.X, op=ALU.add), [osb], [gs])
                V(lambda e: e.tensor_reduce(gs.ap[:, 8:16], q3, axis=AX.X, op=ALU.add), [osq], [gs])
                V(lambda e: e.tensor_scalar_mul(gs.ap[:, 0:8], gs.ap[:, 0:8], 1.0 / 64), [gs], [gs])
                V(lambda e: e.tensor_mul(gs.ap[:, 16:24], gs.ap[:, 0:8], gs.ap[:, 0:8]), [gs], [gs])
                V(lambda e: e.scalar_tensor_tensor(gs.ap[:, 8:16], gs.ap[:, 8:16], 1.0 / 64, gs.ap[:, 16:24],
                                                   op0=ALU.mult, op1=ALU.subtract), [gs], [gs])
                A(lambda e: e.activation(gs.ap[:, 16:24], gs.ap[:, 8:16], AF.Sqrt, bias=epsc.ap[:, 0:1]), [gs, epsc], [gs])
                V(lambda e: e.reciprocal(gs.ap[:, 24:32], gs.ap[:, 16:24]), [gs], [gs])
                V(lambda e: e.tensor_tensor(o3, o3, gs.ap[:, 0:8].unsqueeze(2).to_broadcast([128, 8, 64]),
                                            op=ALU.subtract), [osb, gs], [osb])
                V(lambda e: e.tensor_tensor(o3, o3, gs.ap[:, 24:32].unsqueeze(2).to_broadcast([128, 8, 64]),
                                            op=ALU.mult), [osb, gs], [osb])
                V(lambda e: e.tensor_mul(osq.ap, osb.ap, sg[t].ap), [osb, sg[t]], [osq])

            def stD(t):
                tsl = slice(t * 128, (t + 1) * 128)
                yrt = osql[t % 2]
                btr2 = nb()
                btv2 = btr2.ap
                for j in range(4):
                    T(lambda e, j=j, btv2=btv2: e.transpose(btv2[:, j * 128:(j + 1) * 128], yrt.ap[:, j * 128:(j + 1) * 128],
                                                             ident.ap), [yrt, ident], [btr2])
                A(lambda e, tsl=tsl, btv2=btv2: e.activation(
                    yT.ap[:, 4:8, tsl], btv2[:, 0:512].rearrange("p (a b) -> p a b", a=4), AF.Copy), [btr2], [yT])

            def stE(t):
                tsl = slice(t * 128, (t + 1) * 128)
                bA, bB = nb(), nb()
                st[t]["bAB"] = (bA, bB)
                for hf, bk in enumerate((bA, bB)):
                    for k in range(8):
                        T(lambda e, k=k, hf=hf, bk=bk, tsl=tsl, wo=wo: e.matmul(
                            bk.ap, yT.ap[:, k, tsl], wo[:, k, hf * 512:(hf + 1) * 512], start=(k == 0), stop=(k == 7)),
                          [yT, so], [bk])

            def stF(t):
                c = g * gsz + t
                bA, bB = st[t]["bAB"]
                layer_norm_store(l, 0, bA, bB, xrm[t], gt, lng_t, lnb_t, tb[t % 2], stat[t % 2],
                                 tiles_dst[c], "st%d" % (t % 2), store_eng=SP)

            stages = [stA, stB, stC, stD, stE, stF]
            for step in range(gsz + len(stages) - 1):
                for si in range(len(stages) - 1, -1, -1):
                    t = step - si
                    if 0 <= t < gsz:
                        stages[si](t)

    def ffn(l, tiles_src, tiles_dst, nt_total, which):
        phase_switch()
        if stop[0]:
            return
        gsz = min(NTF, nt_total)
        ng = nt_total // gsz
        NTOK = gsz * 128
        gt = AT([1024], F32, "f_gt")
        lng_t = AT([1024], F32, "f_lng")
        lnb_t = AT([1024], F32, "f_lnb")
        hTs = [AT([8, NTOK], BF16, "f_hT%d" % i) for i in range(2)]
        aT = AT([32, NTOK], BF16, "f_aT")
        t1 = [AT([1024], F32, "f_t1_%d" % i) for i in range(gsz)]
        rl = [AT([NTOK], F32, "f_rl%d" % i) for i in range(2)]
        stat = [AT([24], F32, "f_stat%d" % i) for i in range(2)]
        xr4 = [xr[0], xr[1]] + [AT([1024], F32, "f_xr%d" % i) for i in range(2, gsz)]
        screp = TB(t1[1].ap.bitcast(BF16)[:, 0:1024].rearrange("p (a b) -> p a b", a=8), "f_screp")
        screp.b = t1[1].b
        gate_table(l, 5, which, gt, screp, t1[0])
        row_table(lng_t, lng_d[l, 1:2, :], "lng")
        row_table(lnb_t, lnb_d[l, 1:2, :], "lnb")
        w1all = wff1_d[l, :, :].rearrange("(k p) n -> p k n", p=128)
        w2all = wff2_d[l, :, :].rearrange("(k p) n -> p k n", p=128)
        hTbs = [mk_hTb(gsz, "fhT%d" % i) for i in range(2)]
        alltiles = list(range(gsz))

        def grp(g):
            return tiles_src[g * gsz:(g + 1) * gsz]

        make_hT(grp(0), gsz, which, 3, 4, hTs[0], hTbs[0], next_tiles=grp(1) if ng > 1 else None)
        for g in range(ng):
            hT, hTb = hTs[g % 2], hTbs[g % 2]
            for pc in range(4):
                wv, ws = load_w(w1all[:, :, pc * 1024:(pc + 1) * 1024], [8, 1024])
                for j in range(8):
                    fc = pc * 8 + j
                    bk = nb()
                    for k in range(8):
                        T(lambda e, k=k, j=j, wv=wv, bk=bk, hT=hT: e.matmul(
                            bk.ap[:, 0:NTOK], wv[:, k, j * 128:(j + 1) * 128], hT.ap[:, k, :], start=(k == 0), stop=(k == 7)),
                          [ws] + hT_reads(hTb, alltiles, k), [bk])
                    r = rl[fc % 2]
                    A(lambda e, bk=bk, r=r: e.activation(r.ap, bk.ap[:, 0:NTOK], AF.Relu), [bk], [r])
                    A(lambda e, fc=fc, r=r: e.activation(aT.ap[:, fc, :], r.ap, AF.Square), [r], [aT])
                if g + 1 < ng and pc < gsz:
                    make_hT(grp(g + 1), gsz, which, 3, 4, hTs[(g + 1) % 2], hTbs[(g + 1) % 2], only=[pc],
                            next_tiles=(grp(g + 2) if (pc == gsz - 1 and g + 2 < ng) else None))
            for t in range(gsz):
                c = g * gsz + t
                ld(xr4[t], tiles_src[c].ap, "fxr%d" % t, src_tb=tiles_src[c])
            for q in range(4):
                wv, ws = load_w(w2all[:, :, q * 256:(q + 1) * 256], [32, 256])
                for t in range(gsz):
                    bk = nb()
                    for k in range(32):
                        T(lambda e, k=k, t=t, wv=wv, bk=bk: e.matmul(
                            bk.ap[:, 0:256], aT.ap[:, k, t * 128:(t + 1) * 128], wv[:, k, :], start=(k == 0), stop=(k == 31)),
                          [ws, aT], [bk])
                    V(lambda e, q=q, t=t, bk=bk: e.tensor_mul(t1[t].ap[:, q * 256:(q + 1) * 256], bk.ap[:, 0:256],
                                                               gt.ap[:, q * 256:(q + 1) * 256]), [bk, gt], [t1[t]])
                    if q == 3:
                        c = g * gsz + t
                        ln_tail(xr4[t], t1[t], lng_t, lnb_t, stat[t % 2], tiles_dst[c], "fst%d" % t, aff=V, store_eng=SP)

    lat_src = xT
    ctx_src = ctxT
    last_out = None
    for l in range(2):
        last = (l == 1)
        layer_tables(l)
        if not stop[0]:
            modulation(l)
        cf, cb = pass1(l, ctx_src, 2, 1, zst, zst, SFc, SBc, True, hTcG)
        V(lambda e, cf=cf: e.tensor_copy(cinf.ap, cf.ap), [cf], [cinf])
        V(lambda e, cb=cb: e.tensor_copy(cinb.ap, cb.ap), [cb], [cinb])
        if not last:
            mixer(l, ctx_src, xcmT, 2, 1, SFc, SBc, 256, hTcG)
            ffn(l, xcmT, xc1T, 2, 1)
        pass1(l, lat_src, 32, 0, cinf, cinb, SF, SB, True, hTlG)
        mixer(l, lat_src, x1T, 32, 0, SF, SB, 64, hTlG)
        dst = yT_ if last else x2T
        ffn(l, x1T, dst, 32, 0)
        lat_src = x2T
        ctx_src = xc1T
    finals = [dst_t.b.w for dst_t in yT_ if dst_t.b.w is not None]
    if stop[0]:
        for lst in (x1T, x2T, xcmT, xc1T):
            finals += [t.b.w for t in lst if t.b.w is not None and t.b.w.is_dma]
    p.finalize(final_waits=finals)
    return nc


_CACHE = {}


def _prep_inputs(inp, b):
    f = np.float32
    c = np.asarray(inp["c"], f)[b]
    cctx = np.asarray(inp["c_ctx"], f)
    cc = np.stack([c.reshape(8, 128).T, cctx.reshape(8, 128).T], axis=-1)
    b_ada = np.asarray(inp["b_ada"], f)
    b_col = np.ascontiguousarray(b_ada.reshape(2, 48, 128).transpose(2, 0, 1))
    conv_w = np.asarray(inp["conv_w"], f)
    cw = np.ascontiguousarray(conv_w.reshape(2, 3, 2, 128).transpose(3, 0, 2, 1))
    ws = np.asarray(inp["gmlp_ws"], f)
    wsT = np.ascontiguousarray(ws.transpose(0, 3, 1, 2))
    bs = np.asarray(inp["gmlp_bs"], f)
    bsT = np.ascontiguousarray(bs.transpose(2, 0, 1))
    dec = np.concatenate([np.asarray(inp["ret_decay_fwd"], f).reshape(-1),
                          np.asarray(inp["ret_decay_bwd"], f).reshape(-1)])
    dec = np.ascontiguousarray(np.broadcast_to(dec[None, :], (128, 32)))
    return {
        "x": np.ascontiguousarray(np.asarray(inp["x"], f)[b]),
        "ctx": np.ascontiguousarray(np.asarray(inp["ctx"], f)[b]),
        "cc": np.ascontiguousarray(cc),
        "w_ada": np.asarray(inp["w_ada"], f),
        "b_col": b_col,
        "b_ada": b_ada,
        "w_in": np.asarray(inp["w_in"], f),
        "cw": cw,
        "gln_g": np.asarray(inp["gmlp_ln_g"], f),
        "gln_b": np.asarray(inp["gmlp_ln_b"], f),
        "wsT": wsT,
        "bsT": bsT,
        "dec": dec,
        "w_out": np.asarray(inp["w_out"], f),
        "w_ff1": np.asarray(inp["w_ff1"], f),
        "w_ff2": np.asarray(inp["w_ff2"], f),
        "ln_g": np.asarray(inp["ln_g"], f),
        "ln_b": np.asarray(inp["ln_b"], f),
    }


def kernel(**inputs):
    if "nc" not in _CACHE:
        _CACHE["nc"] = build_program()
    nc = _CACHE["nc"]
    in_maps = [_prep_inputs(inputs, b) for b in range(8)]
    res = run_bass_kernel_spmd(nc, in_maps, core_ids=list(range(8)))
    return np.stack([np.asarray(r["y"], np.float32) for r in res.results], axis=0)
```

```python
import contextlib
import os
import numpy as np
import concourse.bass as bass
import concourse.mybir as mybir
from concourse.bass_utils import run_bass_kernel_spmd

F32 = mybir.dt.float32
BF16 = mybir.dt.bfloat16
AF = mybir.ActivationFunctionType
ALU = mybir.AluOpType
AX = mybir.AxisListType

PE, ACT, DVE, POOL, SP = "tensor", "scalar", "vector", "gpsimd", "sync"
ENGS = (PE, ACT, DVE, POOL, SP)

D = 1024
SEQ = 4096
CTXL = 256
DIN = 3328
DFF = 4096
ALPHA = float(4.0 ** 0.25)
EPS = 1e-5
NTM = 4
NTF = 4


class Buf:
    __slots__ = ("name", "w", "rs", "excl")

    def __init__(self, name, excl=False):
        self.name = name
        self.w = None
        self.rs = []
        self.excl = excl


class Op:
    __slots__ = ("eng", "fn", "deps", "needs_inc", "count", "dma_key", "is_dma", "pos")

    def __init__(self, eng, fn, is_dma=False, dma_key=None):
        self.eng = eng
        self.fn = fn
        self.deps = []
        self.needs_inc = False
        self.count = 0
        self.is_dma = is_dma
        self.dma_key = dma_key
        self.pos = 0


class Prog:
    def __init__(self, nc):
        self.nc = nc
        self.streams = {e: [] for e in ENGS}
        self.dma_counts = {}
        self.last_dma = {}

    @staticmethod
    def _collapse(ops):
        last = {}
        out = []
        for r in ops:
            if r.is_dma:
                out.append(r)
            elif r.eng not in last or last[r.eng].pos < r.pos:
                last[r.eng] = r
        return out + list(last.values())

    def _add_deps(self, op, reads, writes):
        deps = op.deps
        for b in reads:
            if b.w is not None:
                deps.append((b.w, "raw"))
            if b.excl:
                b.rs = self._collapse(b.rs)
                for r in b.rs:
                    deps.append((r, "order"))
            b.rs.append(op)
        for b in writes:
            if b.w is not None:
                deps.append((b.w, "order"))
            for r in self._collapse(b.rs):
                if r is not op:
                    deps.append((r, "order"))
            b.w = op
            b.rs = []

    def op(self, eng, fn, reads=(), writes=()):
        o = Op(eng, fn)
        self._add_deps(o, reads, writes)
        o.pos = len(self.streams[eng])
        self.streams[eng].append(o)
        return o

    def dma(self, eng, out, in_, key, reads=(), writes=(), **kw):
        def fn(e, out=out, in_=in_, kw=kw):
            return e.dma_start(out=out, in_=in_, **kw)
        o = Op(eng, fn, is_dma=True, dma_key=key)
        self._add_deps(o, reads, writes)
        if key in self.last_dma:
            o.deps.append((self.last_dma[key], "order"))
        self.last_dma[key] = o
        self.dma_counts[key] = self.dma_counts.get(key, 0) + 1
        o.count = 16 * self.dma_counts[key]
        o.pos = len(self.streams[eng])
        self.streams[eng].append(o)
        return o

    def finalize(self, final_waits=()):
        nc = self.nc
        for e in ENGS:
            for o in self.streams[e]:
                real = []
                for (p, kind) in o.deps:
                    if p.is_dma:
                        real.append(p)
                        continue
                    if (not o.is_dma) and p.eng == o.eng:
                        if e != PE:
                            real.append(p)
                        continue
                    real.append(p)
                o.deps = real
                for p in real:
                    if not p.is_dma:
                        p.needs_inc = True
        for e in ENGS:
            c = 0
            for o in self.streams[e]:
                if o.is_dma:
                    continue
                if o.needs_inc:
                    c += 1
                    o.count = c
        with contextlib.ExitStack() as es:
            esem = {e: es.enter_context(nc.semaphore("s_" + e)) for e in (PE, ACT, DVE, POOL)}
            dsem = {k: es.enter_context(nc.semaphore("d_%s" % (k,))) for k in self.dma_counts}
            block = es.enter_context(nc.Block())

            def emit(e, eng):
                waited = {}
                for o in self.streams[e]:
                    need = {}
                    for p in o.deps:
                        k = ("d", p.dma_key) if p.is_dma else ("e", p.eng)
                        s = dsem[p.dma_key] if p.is_dma else esem[p.eng]
                        if need.get(k, (None, 0))[1] < p.count:
                            need[k] = (s, p.count)
                    for k, (s, v) in need.items():
                        if waited.get(k, 0) >= v:
                            continue
                        waited[k] = v
                        eng.wait_ge(s, v)
                    ins = o.fn(eng)
                    if o.is_dma:
                        ins.then_inc(dsem[o.dma_key], 16)
                    elif o.needs_inc:
                        ins.then_inc(esem[e], 1)
                if e == SP:
                    for p in final_waits:
                        s = dsem[p.dma_key] if p.is_dma else esem[p.eng]
                        eng.wait_ge(s, p.count)

            @block.sync
            def _(eng):
                emit(SP, eng)

            @block.scalar
            def _(eng):
                emit(ACT, eng)

            @block.vector
            def _(eng):
                emit(DVE, eng)

            @block.gpsimd
            def _(eng):
                emit(POOL, eng)

            @block.tensor
            def _(eng):
                emit(PE, eng)


class TB:
    __slots__ = ("ap", "b")

    def __init__(self, ap, name, excl=False):
        self.ap = ap
        self.b = Buf(name, excl)


class Arena:
    def __init__(self, base_ap_f32, nwords):
        self.base = base_ap_f32
        self.nwords = nwords
        self.off = 0

    def reset(self):
        self.off = 0

    def get(self, shape, dt, name="a"):
        n = 1
        for s in shape:
            n *= s
        nbytes = n * (2 if dt == BF16 else 4)
        nw = (nbytes + 3) // 4
        nw = (nw + 7) // 8 * 8
        assert self.off + nw <= self.nwords, ("arena overflow", name, self.off, nw, self.nwords)
        ap = self.base[:, self.off:self.off + (nbytes + 3) // 4]
        self.off += nw
        if dt == BF16:
            ap = ap.bitcast(BF16)
        if len(shape) == 2:
            ap = ap.rearrange("p (a b) -> p a b", a=shape[0])
        elif len(shape) == 3:
            ap = ap.rearrange("p (a b c) -> p a b c", a=shape[0], b=shape[1])
        return ap


def build_program(kstop=None):
    nc = bass.Bass("TRN2", target_bir_lowering=False)
    p = Prog(nc)

    def din(name, shape):
        return nc.dram_tensor(name, list(shape), F32, kind="ExternalInput")

    x_d = din("x", [SEQ, D])
    ctx_d = din("ctx", [CTXL, D])
    cc_d = din("cc", [128, 8, 2])
    wada_d = din("w_ada", [2, D, 6 * D])
    bcol_d = din("b_col", [128, 2, 48])
    bada_d = din("b_ada", [2, 6 * D])
    win_d = din("w_in", [2, D, DIN])
    cw_d = din("cw", [128, 2, 2, 3])
    glng_d = din("gln_g", [2, 256])
    glnb_d = din("gln_b", [2, 256])
    wsT_d = din("wsT", [2, 128, 4, 128])
    bsT_d = din("bsT", [128, 2, 4])
    dec_d = din("dec", [128, 32])
    wout_d = din("w_out", [2, D, D])
    wff1_d = din("w_ff1", [2, D, DFF])
    wff2_d = din("w_ff2", [2, DFF, D])
    lng_d = din("ln_g", [2, 2, D])
    lnb_d = din("ln_b", [2, 2, D])
    y_d = nc.dram_tensor("y", [SEQ, D], F32, kind="ExternalOutput")
    x1_d = nc.dram_tensor("x1buf", [SEQ, D], F32)
    x2_d = nc.dram_tensor("x2buf", [SEQ, D], F32)
    xcm_d = nc.dram_tensor("xcm", [CTXL, D], F32)
    xc1_d = nc.dram_tensor("xc1", [CTXL, D], F32)
    hTl_d = nc.dram_tensor("hTl", [8, 128, 8 * 512], BF16)
    hTc_d = nc.dram_tensor("hTc", [1, 128, 8 * 256], BF16)

    def dram_tiles(t, n):
        return [TB(t[i * 128:(i + 1) * 128, :], "%s_%d" % (t.name, i)) for i in range(n)]

    xT = dram_tiles(x_d, 32)
    ctxT = dram_tiles(ctx_d, 2)
    yT_ = dram_tiles(y_d, 32)
    x1T = dram_tiles(x1_d, 32)
    x2T = dram_tiles(x2_d, 32)
    xcmT = dram_tiles(xcm_d, 2)
    xc1T = dram_tiles(xc1_d, 2)
    hTlG = [TB(hTl_d[g, :, :].rearrange("p (k n) -> p k n", k=8), "hTl%d" % g) for g in range(8)]
    hTcG = [TB(hTc_d[0, :, :].rearrange("p (k n) -> p k n", k=8), "hTc0")]

    def sbt(name, shape, dt=F32):
        t = nc.alloc_sbuf_tensor("sb_" + name, list(shape), dt)
        return t

    def full(t):
        nd = len(t.shape)
        return t[tuple(slice(None) for _ in range(nd))]

    banks = []
    for i in range(8):
        t = nc.alloc_psum_tensor("bank%d" % i, [128, 512], F32)
        banks.append(TB(t[:, :], "bank%d" % i, excl=True))
    bank_ctr = [0]

    def nb():
        b = banks[bank_ctr[0] % 8]
        bank_ctr[0] += 1
        return b

    NSLOT = 3
    wslots = []
    for i in range(NSLOT):
        t = sbt("wslot%d" % i, [128, 8192], BF16)
        wslots.append(TB(t[:, :], "wslot%d" % i))
    slot_ctr = [0]

    def load_w(src_ap, shape, cast=True):
        s = wslots[slot_ctr[0] % NSLOT]
        key = ("w%d" if cast else "wf%d") % (slot_ctr[0] % NSLOT)
        slot_ctr[0] += 1
        n = 1
        for v in shape:
            n *= v
        if cast:
            view = s.ap[:, 0:n]
        else:
            view = s.ap.bitcast(F32)[:, 0:n]
        if len(shape) == 2:
            view = view.rearrange("p (a b) -> p a b", a=shape[0])
        p.dma(POOL if cast else SP, view, src_ap, key=key, writes=[s.b])
        return view, s

    xl = []
    xr = []
    for i in range(2):
        t = sbt("xl%d" % i, [128, D])
        xl.append(TB(t[:, :], "xl%d" % i))
        t = sbt("xr%d" % i, [128, D])
        xr.append(TB(t[:, :], "xr%d" % i))
    pending_x = {}

    def prefetch_x(tiles, t_list):
        for t in t_list:
            if t < len(tiles):
                ld(xl[t % 2], tiles[t].ap, "xl%d" % (t % 2), src_tb=tiles[t])
                pending_x[id(tiles[t])] = t % 2

    rel = TB(full(sbt("rel", [128, 128])), "rel")
    ident = TB(full(sbt("ident", [128, 128])), "ident")
    identb = TB(full(sbt("identb", [128, 128], BF16)), "identb")
    iota1 = TB(full(sbt("iota1", [128, 128])), "iota1")
    iotar = TB(full(sbt("iotar", [128, 128])), "iotar")
    pcol = TB(full(sbt("pcol", [128, 1])), "pcol")
    pcolr = TB(full(sbt("pcolr", [128, 1])), "pcolr")
    lgall = TB(full(sbt("lgall", [128, 32])), "lgall")
    DT = TB(full(sbt("DT", [128, 8, 128], BF16)), "DT")
    dqf = TB(full(sbt("dqf", [128, 4, 128])), "dqf")
    dqb = TB(full(sbt("dqb", [128, 4, 128])), "dqb")
    dkf = TB(full(sbt("dkf", [128, 8])), "dkf")
    dkb = TB(full(sbt("dkb", [128, 8])), "dkb")
    lgself = TB(full(sbt("lgself", [128, 4])), "lgself")
    lgselb = TB(full(sbt("lgselb", [128, 4])), "lgselb")
    Gself = TB(full(sbt("Gself", [128, 4])), "Gself")
    Gselb = TB(full(sbt("Gselb", [128, 4])), "Gselb")
    Gtf = TB(full(sbt("Gtf", [128, 4, 64])), "Gtf")
    Gtb = TB(full(sbt("Gtb", [128, 4, 64])), "Gtb")
    tmp8 = TB(full(sbt("tmp8", [128, 8])), "tmp8")
    barr = TB(full(sbt("barr", [128, 1])), "barr")
    modcol = TB(full(sbt("modcol", [128, 48, 2])), "modcol")
    bcol = TB(full(sbt("bcol", [128, 2, 48])), "bcol")
    ccs = TB(full(sbt("ccs", [128, 8, 2])), "ccs")
    scs = TB(full(sbt("scs", [128, 8, 2])), "scs")
    scb = TB(full(sbt("scb", [128, 8, 2], BF16)), "scb")
    cwt = TB(full(sbt("cwt", [128, 2, 2, 3])), "cwt")
    glng = TB(full(sbt("glng", [128, 256])), "glng")
    glnb = TB(full(sbt("glnb", [128, 256])), "glnb")
    wsTb = TB(full(sbt("wsTb", [128, 4, 128], BF16)), "wsTb")
    bsT = TB(full(sbt("bsT", [128, 2, 4])), "bsT")
    SF = TB(full(sbt("SF", [128, 33, 4, 64], BF16)), "SF")
    SB = TB(full(sbt("SB", [128, 33, 4, 64], BF16)), "SB")
    SFc = TB(full(sbt("SFc", [128, 3, 4, 64], BF16)), "SFc")
    SBc = TB(full(sbt("SBc", [128, 3, 4, 64], BF16)), "SBc")
    cinf = TB(full(sbt("cinf", [128, 4, 64])), "cinf")
    cinb = TB(full(sbt("cinb", [128, 4, 64])), "cinb")
    zst = TB(full(sbt("zst", [128, 4, 64])), "zst")

    ARENA_W = 23040
    arena_t = sbt("arena", [128, ARENA_W])
    ar = Arena(arena_t[:, :], ARENA_W)
    phase_tok = [0]

    def V(fn, reads=(), writes=()):
        return p.op(DVE, fn, [r.b for r in reads], [w.b for w in writes])

    def A(fn, reads=(), writes=()):
        return p.op(ACT, fn, [r.b for r in reads], [w.b for w in writes])

    def G(fn, reads=(), writes=()):
        return p.op(POOL, fn, [r.b for r in reads], [w.b for w in writes])

    def T(fn, reads=(), writes=()):
        return p.op(PE, fn, [r.b for r in reads], [w.b for w in writes])

    def ld(dst, src_ap, key, src_tb=None):
        return p.dma(SP, dst.ap, src_ap, key=key, reads=[src_tb.b] if src_tb is not None else [], writes=[dst.b])

    arena_bufs = []

    last_barrier = [None]

    def AT(shape, dt, name):
        tb = TB(ar.get(shape, dt, name), name)
        tb.b.w = last_barrier[0]
        arena_bufs.append(tb)
        return tb

    stop = [False]
    phase_no = [0]

    def phase_switch():
        phase_no[0] += 1
        if kstop is not None and phase_no[0] > kstop:
            stop[0] = True
        if arena_bufs:
            last_barrier[0] = V(lambda e: e.memset(barr.ap, 0.0), writes=list(arena_bufs) + [barr])
        del arena_bufs[:]
        ar.reset()

    G(lambda e: e.iota(rel.ap, [[1, 128]], base=0, channel_multiplier=-1, allow_small_or_imprecise_dtypes=True),
      writes=[rel])
    G(lambda e: e.iota(iota1.ap, [[1, 128]], base=1, channel_multiplier=0, allow_small_or_imprecise_dtypes=True),
      writes=[iota1])
    G(lambda e: e.iota(iotar.ap, [[-1, 128]], base=128, channel_multiplier=0, allow_small_or_imprecise_dtypes=True),
      writes=[iotar])
    G(lambda e: e.iota(pcol.ap, [[0, 1]], base=0, channel_multiplier=1, allow_small_or_imprecise_dtypes=True),
      writes=[pcol])
    G(lambda e: e.iota(pcolr.ap, [[0, 1]], base=127, channel_multiplier=-1, allow_small_or_imprecise_dtypes=True),
      writes=[pcolr])
    V(lambda e: e.tensor_single_scalar(ident.ap, rel.ap, 0.0, op=ALU.is_equal), [rel], [ident])
    V(lambda e: e.tensor_copy(identb.ap, ident.ap), [ident], [identb])
    V(lambda e: e.memset(zst.ap, 0.0), writes=[zst])
    V(lambda e: e.memset(SFc.ap, 0.0), writes=[SFc])
    V(lambda e: e.memset(SBc.ap, 0.0), writes=[SBc])
    ld(lgall, dec_d[:, :], "misc")
    ld(bcol, bcol_d[:, :, :], "misc")
    ld(ccs, cc_d[:, :, :], "misc")
    ld(cwt, cw_d[:, :, :, :], "misc")
    ld(bsT, bsT_d[:, :, :], "misc")
    A(lambda e: e.activation(lgall.ap, lgall.ap, AF.Exp), [lgall], [lgall])
    V(lambda e: e.tensor_scalar_mul(lgall.ap, lgall.ap, -1.0), [lgall], [lgall])
    A(lambda e: e.activation(scs.ap, ccs.ap, AF.Silu), [ccs], [scs])
    V(lambda e: e.tensor_copy(scb.ap, scs.ap), [scs], [scb])

    def layer_tables(l):
        phase_switch()
        if stop[0]:
            return
        relpos = AT([128], F32, "relpos")
        relneg = AT([128], F32, "relneg")
        maskf = AT([128], F32, "maskf")
        maskb = AT([128], F32, "maskb")
        tmpA = AT([128], F32, "tmpA")
        tmpB = AT([128], F32, "tmpB")
        V(lambda e: e.tensor_single_scalar(maskf.ap, rel.ap, 0.0, op=ALU.is_ge), [rel], [maskf])
        V(lambda e: e.tensor_single_scalar(maskb.ap, rel.ap, 0.0, op=ALU.is_lt), [rel], [maskb])
        V(lambda e: e.tensor_scalar_max(relpos.ap, rel.ap, 0.0), [rel], [relpos])
        V(lambda e: e.tensor_sub(relneg.ap, relpos.ap, rel.ap), [relpos, rel], [relneg])
        lgf = lgall.ap[:, l * 8:(l + 1) * 8]
        lgb = lgall.ap[:, 16 + l * 8:16 + (l + 1) * 8]
        for h in range(8):
            A(lambda e, h=h: e.activation(tmpA.ap, relpos.ap, AF.Exp, scale=lgf[:, h:h + 1]), [relpos, lgall], [tmpA])
            A(lambda e, h=h: e.activation(tmpB.ap, relneg.ap, AF.Exp, scale=lgb[:, h:h + 1]), [relneg, lgall], [tmpB])
            V(lambda e: e.tensor_mul(tmpA.ap, tmpA.ap, maskf.ap), [tmpA, maskf], [tmpA])
            V(lambda e: e.tensor_mul(tmpB.ap, tmpB.ap, maskb.ap), [tmpB, maskb], [tmpB])
            V(lambda e, h=h: e.tensor_add(DT.ap[:, h, :], tmpA.ap, tmpB.ap), [tmpA, tmpB], [DT])
        for (sel, lg) in ((lgself, lgf), (lgselb, lgb)):
            V(lambda e, sel=sel, lg=lg: e.tensor_copy(sel.ap[0:64, :], lg[0:64, 0::2]), [lgall], [sel])
            V(lambda e, sel=sel, lg=lg: e.tensor_copy(sel.ap[64:128, :], lg[64:128, 1::2]), [lgall], [sel])
        for pr in range(4):
            A(lambda e, pr=pr: e.activation(tmpA.ap, iota1.ap, AF.Exp, scale=lgself.ap[:, pr:pr + 1]),
              [iota1, lgself], [tmpA])
            V(lambda e, pr=pr: e.tensor_scalar_mul(dqf.ap[:, pr, :], tmpA.ap, 0.125), [tmpA], [dqf])
            A(lambda e, pr=pr: e.activation(tmpB.ap, iotar.ap, AF.Exp, scale=lgselb.ap[:, pr:pr + 1]),
              [iotar, lgselb], [tmpB])
            V(lambda e, pr=pr: e.tensor_scalar_mul(dqb.ap[:, pr, :], tmpB.ap, 0.125), [tmpB], [dqb])
        V(lambda e: e.tensor_scalar_mul(tmp8.ap, lgf, pcolr.ap[:, 0:1]), [lgall, pcolr], [tmp8])
        A(lambda e: e.activation(dkf.ap, tmp8.ap, AF.Exp), [tmp8], [dkf])
        V(lambda e: e.tensor_scalar_mul(tmp8.ap, lgb, pcol.ap[:, 0:1]), [lgall, pcol], [tmp8])
        A(lambda e: e.activation(dkb.ap, tmp8.ap, AF.Exp), [tmp8], [dkb])
        A(lambda e: e.activation(Gself.ap, lgself.ap, AF.Exp, scale=128.0), [lgself], [Gself])
        A(lambda e: e.activation(Gselb.ap, lgselb.ap, AF.Exp, scale=128.0), [lgselb], [Gselb])
        V(lambda e: e.tensor_copy(Gtf.ap, Gself.ap.unsqueeze(2).to_broadcast([128, 4, 64])), [Gself], [Gtf])
        V(lambda e: e.tensor_copy(Gtb.ap, Gselb.ap.unsqueeze(2).to_broadcast([128, 4, 64])), [Gselb], [Gtb])
        ld(glng, glng_d[l:l + 1, :].to_broadcast([128, 256]), "misc")
        ld(glnb, glnb_d[l:l + 1, :].to_broadcast([128, 256]), "misc")
        wsTf = AT([4, 128], F32, "wsTf")
        ld(wsTf, wsT_d[l, :, :, :], "misc")
        V(lambda e: e.tensor_copy(wsTb.ap, wsTf.ap), [wsTf], [wsTb])

    def modulation(l):
        bk = nb()
        wv_all = wada_d[l, :, :].rearrange("(k p) n -> p k n", p=128)
        for pc in range(12):
            wv, ws = load_w(wv_all[:, :, pc * 512:(pc + 1) * 512], [8, 512], cast=True)
            for jj in range(4):
                j = pc * 4 + jj
                for k in range(8):
                    T(lambda e, j=j, jj=jj, k=k, wv=wv: e.matmul(
                        bk.ap[:, 2 * j:2 * j + 2], wv[:, k, jj * 128:(jj + 1) * 128], scb.ap[:, k, :],
                        start=(k == 0), stop=(k == 7)), [ws, scb], [bk])
        V(lambda e: e.tensor_tensor(modcol.ap, bk.ap[:, 0:96].rearrange("p (a b) -> p a b", b=2),
                                    bcol.ap[:, l, :].unsqueeze(2).to_broadcast([128, 48, 2]), op=ALU.add),
          [bk, bcol], [modcol])
        V(lambda e: e.tensor_scalar_add(modcol.ap[:, 8:16, :], modcol.ap[:, 8:16, :], 1.0), [modcol], [modcol])
        V(lambda e: e.tensor_scalar_add(modcol.ap[:, 32:40, :], modcol.ap[:, 32:40, :], 1.0), [modcol], [modcol])

    def gate_table(l, m, which, dst, screp, brow):
        V(lambda e: e.tensor_copy(screp.ap, scs.ap[:, :, which:which + 1].to_broadcast([128, 8, 128])),
          [scs], [screp])
        ld(brow, bada_d[l:l + 1, m * 1024:(m + 1) * 1024].to_broadcast([128, 1024]), "brow")
        wv_all = wada_d[l, :, :].rearrange("(k p) n -> p k n", p=128)
        for hf in range(2):
            c0 = m * 1024 + hf * 512
            wv, ws = load_w(wv_all[:, :, c0:c0 + 512], [8, 512], cast=True)
            bk = nb()
            for k in range(8):
                T(lambda e, k=k, wv=wv, bk=bk: e.matmul(bk.ap, screp.ap[:, k, :], wv[:, k, :],
                                                          start=(k == 0), stop=(k == 7)), [ws, screp], [bk])
            V(lambda e, hf=hf, bk=bk: e.tensor_add(dst.ap[:, hf * 512:(hf + 1) * 512], bk.ap,
                                                    brow.ap[:, hf * 512:(hf + 1) * 512]), [bk, brow], [dst])

    def gelu(bk, dst, tmp=None):
        A(lambda e: e.activation(dst.ap, bk.ap[:, 0:256], AF.Gelu_apprx_tanh), [bk], [dst])

    def row_table(dst, src_row_ap, key):
        ld(dst, src_row_ap.to_broadcast([128, 1024]), key)

    def make_hT(tiles_src, nt, which, m_shift, m_scale, hT, hTb, next_tiles=None, only=None):
        for t in (range(nt) if only is None else only):
            xs = xl[t % 2]
            if pending_x.pop(id(tiles_src[t]), None) is None:
                ld(xs, tiles_src[t].ap, "xl%d" % (t % 2), src_tb=tiles_src[t])
            for kg in range(2):
                bk = nb()
                for kk in range(4):
                    k = kg * 4 + kk
                    T(lambda e, k=k, kk=kk, bk=bk, xs=xs: e.transpose(
                        bk.ap[:, kk * 128:(kk + 1) * 128], xs.ap[:, k * 128:(k + 1) * 128], ident.ap),
                      [xs, ident], [bk])
                for kk in range(4):
                    k = kg * 4 + kk
                    sc = modcol.ap[:, m_scale * 8 + k, which:which + 1]
                    sh = modcol.ap[:, m_shift * 8 + k, which:which + 1]
                    o_ap = hT.ap[:, k, t * 128:(t + 1) * 128]
                    i_ap = bk.ap[:, kk * 128:(kk + 1) * 128]
                    if kk % 2 == 0:
                        A(lambda e, o_ap=o_ap, i_ap=i_ap, sc=sc, sh=sh: e.activation(
                            o_ap, i_ap, AF.Identity, bias=sh, scale=sc), [bk, modcol], [hTb[t][kg]])
                    else:
                        V(lambda e, o_ap=o_ap, i_ap=i_ap, sc=sc, sh=sh: e.tensor_scalar(
                            o_ap, i_ap, sc, sh, op0=ALU.mult, op1=ALU.add), [bk, modcol], [hTb[t][kg]])
        if next_tiles:
            prefetch_x(next_tiles, [0, 1])

    def hT_reads(hTb, tiles, k):
        return [hTb[t][k // 4] for t in tiles]

    class BB:
        def __init__(self, b):
            self.b = b

    def mk_hTb(nt, name):
        r = [[BB(Buf("%s_%d_%d" % (name, t, g))) for g in range(2)] for t in range(nt)]
        for row in r:
            for bb in row:
                bb.b.w = last_barrier[0]
                arena_bufs.append(bb)
        return r

    def pass1(l, tiles_src, nt_total, which, init_f, init_b, SFo, SBo, want_own, hTG):
        phase_switch()
        if stop[0]:
            return zst, zst
        ng = nt_total // min(4, nt_total)
        gsz = min(4, nt_total)
        hT = AT([8, gsz * 128], BF16, "p1_hT")
        dSb = AT([nt_total, 4, 64], F32, "p1_dSb")
        kdf = [AT([512], BF16, "p1_kdf%d" % i) for i in range(2)]
        kdb = [AT([512], BF16, "p1_kdb%d" % i) for i in range(2)]
        vbf = [AT([512], BF16, "p1_v%d" % i) for i in range(2)]
        stf = [AT([4, 64], F32, "p1_stf%d" % i) for i in range(2)]
        stb = [AT([4, 64], F32, "p1_stb%d" % i) for i in range(2)]
        wall = win_d[l, :, :].rearrange("(k p) n -> p k n", p=128)
        wk, wks = load_w(wall[:, :, 1792:2304], [8, 512])
        wvv, wvs = load_w(wall[:, :, 2304:2816], [8, 512])
        cur = 0
        V(lambda e: e.tensor_copy(stf[0].ap, init_f.ap), [init_f], [stf[0]])
        if want_own:
            A(lambda e: e.activation(SFo.ap[:, 0, :, :], stf[0].ap, AF.Copy), [stf[0]], [SFo])
        hTb = mk_hTb(gsz, "p1hT")
        for g in range(ng):
            make_hT(tiles_src[g * gsz:(g + 1) * gsz], gsz, which, 0, 1, hT, hTb,
                    next_tiles=tiles_src[(g + 1) * gsz:(g + 2) * gsz] if g + 1 < ng else None)
            p.dma(POOL, hTG[g].ap, hT.ap, key="hTst", reads=[bb.b for row in hTb for bb in row], writes=[hTG[g].b])
            def partA(t):
                c = g * gsz + t
                i2 = c % 2
                bk_k = nb()
                bk_v = nb()
                for k in range(8):
                    T(lambda e, k=k, t=t, bk_k=bk_k: e.matmul(bk_k.ap, hT.ap[:, k, t * 128:(t + 1) * 128], wk[:, k, :],
                                                                start=(k == 0), stop=(k == 7)),
                      [hTb[t][k // 4], wks], [bk_k])
                for k in range(8):
                    T(lambda e, k=k, t=t, bk_v=bk_v: e.matmul(bk_v.ap, hT.ap[:, k, t * 128:(t + 1) * 128], wvv[:, k, :],
                                                                start=(k == 0), stop=(k == 7)),
                      [hTb[t][k // 4], wvs], [bk_v])
                kview = bk_k.ap.rearrange("p (h d) -> p h d", h=8)
                V(lambda e, i2=i2, kview=kview: e.tensor_tensor(
                    kdf[i2].ap.rearrange("p (h d) -> p h d", h=8), kview,
                    dkf.ap.unsqueeze(2).to_broadcast([128, 8, 64]), op=ALU.mult), [bk_k, dkf], [kdf[i2]])
                V(lambda e, i2=i2, kview=kview: e.tensor_tensor(
                    kdb[i2].ap.rearrange("p (h d) -> p h d", h=8), kview,
                    dkb.ap.unsqueeze(2).to_broadcast([128, 8, 64]), op=ALU.mult), [bk_k, dkb], [kdb[i2]])
                A(lambda e, i2=i2, bk_v=bk_v: e.activation(vbf[i2].ap, bk_v.ap, AF.Copy), [bk_v], [vbf[i2]])

            def partB(t, cur):
                c = g * gsz + t
                i2 = c % 2
                bf_ = nb()
                bb_ = nb()
                for pr in range(4):
                    T(lambda e, pr=pr, i2=i2, bf_=bf_: e.matmul(
                        bf_.ap[:, pr * 128:(pr + 1) * 128], kdf[i2].ap[:, pr * 128:(pr + 1) * 128],
                        vbf[i2].ap[:, pr * 128:(pr + 1) * 128], start=True, stop=True), [kdf[i2], vbf[i2]], [bf_])
                for pr in range(4):
                    T(lambda e, pr=pr, i2=i2, bb_=bb_: e.matmul(
                        bb_.ap[:, pr * 128:(pr + 1) * 128], kdb[i2].ap[:, pr * 128:(pr + 1) * 128],
                        vbf[i2].ap[:, pr * 128:(pr + 1) * 128], start=True, stop=True), [kdb[i2], vbf[i2]], [bb_])
                nxt = 1 - cur
                V(lambda e, cur=cur, nxt=nxt: e.tensor_mul(stf[nxt].ap, stf[cur].ap, Gtf.ap), [stf[cur], Gtf], [stf[nxt]])
                for hl in range(2):
                    ps_ = slice(hl * 64, hl * 64 + 64)
                    dsv = bf_.ap.rearrange("p (a b) -> p a b", a=4)[ps_, :, hl * 64:hl * 64 + 64]
                    V(lambda e, nxt=nxt, ps_=ps_, dsv=dsv: e.tensor_add(stf[nxt].ap[ps_, :, :], stf[nxt].ap[ps_, :, :], dsv),
                      [stf[nxt], bf_], [stf[nxt]])
                    dsv2 = bb_.ap.rearrange("p (a b) -> p a b", a=4)[ps_, :, hl * 64:hl * 64 + 64]
                    A(lambda e, c=c, ps_=ps_, dsv2=dsv2: e.activation(dSb.ap[ps_, c, :, :], dsv2, AF.Copy), [bb_], [dSb])
                cur = nxt
                if want_own or True:
                    if c + 1 < nt_total or True:
                        A(lambda e, c=c, cur=cur: e.activation(SFo.ap[:, c + 1, :, :], stf[cur].ap, AF.Copy),
                          [stf[cur]], [SFo])
                return cur

            for t in range(gsz + 1):
                if t < gsz:
                    partA(t)
                if t >= 1:
                    cur = partB(t - 1, cur)
        fin_f = stf[cur]
        curb = 0
        V(lambda e: e.tensor_copy(stb[0].ap, init_b.ap), [init_b], [stb[0]])
        A(lambda e: e.activation(SBo.ap[:, nt_total - 1, :, :], stb[0].ap, AF.Copy), [stb[0]], [SBo])
        for c in range(nt_total - 1, -1, -1):
            nxt = 1 - curb
            V(lambda e, curb=curb, nxt=nxt: e.tensor_mul(stb[nxt].ap, stb[curb].ap, Gtb.ap), [stb[curb], Gtb], [stb[nxt]])
            V(lambda e, nxt=nxt, c=c: e.tensor_add(stb[nxt].ap, stb[nxt].ap, dSb.ap[:, c, :, :]), [stb[nxt], dSb], [stb[nxt]])
            curb = nxt
            idx = c - 1 if c > 0 else nt_total
            A(lambda e, idx=idx, curb=curb: e.activation(SBo.ap[:, idx, :, :], stb[curb].ap, AF.Copy), [stb[curb]], [SBo])
        fin_b = stb[curb]
        return fin_f, fin_b

    def layer_norm_store(l, which_ln, bkA, bkB, xres, gt, lng_t, lnb_t, tbuf, stat, dst_tile, key, store_eng=None):
        for hf, bk in enumerate((bkA, bkB)):
            sl = slice(hf * 512, (hf + 1) * 512)
            V(lambda e, bk=bk, sl=sl: e.tensor_mul(tbuf.ap[:, sl], bk.ap, gt.ap[:, sl]), [bk, gt], [tbuf])
        ln_tail(xres, tbuf, lng_t, lnb_t, stat, dst_tile, key, store_eng=store_eng)

    def ln_tail(xres, tbuf, lng_t, lnb_t, stat, dst_tile, key, aff=None, store_eng=None):
        V(lambda e: e.scalar_tensor_tensor(tbuf.ap, xres.ap, ALPHA, tbuf.ap, op0=ALU.mult, op1=ALU.add),
          [xres, tbuf], [tbuf])
        for hf in range(2):
            V(lambda e, hf=hf: e.bn_stats(stat.ap[:, hf * 6:(hf + 1) * 6], tbuf.ap[:, hf * 512:(hf + 1) * 512]),
              [tbuf], [stat])
        V(lambda e: e.bn_aggr(stat.ap[:, 12:14], stat.ap[:, 0:12]), [stat], [stat])
        A(lambda e: e.activation(stat.ap[:, 14:15], stat.ap[:, 13:14], AF.Sqrt, bias=epsc.ap[:, 0:1]), [stat, epsc], [stat])
        V(lambda e: e.reciprocal(stat.ap[:, 15:16], stat.ap[:, 14:15]), [stat], [stat])
        V(lambda e: e.scalar_tensor_tensor(stat.ap[:, 16:17], stat.ap[:, 12:13], -1.0, stat.ap[:, 15:16],
                                           op0=ALU.mult, op1=ALU.mult), [stat], [stat])
        A(lambda e: e.activation(tbuf.ap, tbuf.ap, AF.Identity, bias=stat.ap[:, 16:17], scale=stat.ap[:, 15:16]),
          [tbuf, stat], [tbuf])
        aff = aff or V
        aff(lambda e: e.tensor_mul(tbuf.ap, tbuf.ap, lng_t.ap), [tbuf, lng_t], [tbuf])
        aff(lambda e: e.tensor_add(tbuf.ap, tbuf.ap, lnb_t.ap), [tbuf, lnb_t], [tbuf])
        return p.dma(store_eng or ACT, dst_tile.ap, tbuf.ap, key=key, reads=[tbuf.b], writes=[dst_tile.b])

    epsc = TB(full(sbt("epsc", [128, 1])), "epsc")
    V(lambda e: e.memset(epsc.ap, EPS), writes=[epsc])

    def mixer(l, tiles_src, tiles_dst, nt_total, which, SFo, SBo, rowlen, hTG):
        phase_switch()
        if stop[0]:
            return
        gsz = min(NTM, nt_total)
        ng = nt_total // gsz
        NTOK = gsz * 128
        gt = AT([1024], F32, "m_gt")
        lng_t = AT([1024], F32, "m_lng")
        lnb_t = AT([1024], F32, "m_lnb")
        hT = AT([8, NTOK], BF16, "m_hT")
        qT = AT([4, NTOK], BF16, "m_qT")
        qdfT = AT([4, NTOK], BF16, "m_qdf")
        qdbT = AT([4, NTOK], BF16, "m_qdb")
        kT = AT([4, NTOK], BF16, "m_kT")
        yT = AT([8, NTOK], BF16, "m_yT")
        xin = [AT([max(NTOK, 512)], F32, "m_xin%d" % i) for i in range(2)]
        zc = [AT([max(NTOK, 512)], F32, "m_z%d" % i) for i in range(2)]
        yc = xin
        gtmp = None
        ug = [AT([256], F32, "m_ug%d" % i) for i in range(gsz)]
        vgl = [AT([256], F32, "m_vg%d" % i) for i in range(2)]
        vln = [AT([256], BF16, "m_vln%d" % i) for i in range(gsz)]
        vrb = [AT([512], BF16, "m_vr%d" % i) for i in range(gsz)]
        sg = [AT([512], F32, "m_sg%d" % i) for i in range(gsz)]
        PT = [[AT([4, 128], BF16, "m_PT%d_%d" % (i, j)) for j in range(2)] for i in range(2)]
        ygm = [AT([256], F32, "m_ygm%d" % i) for i in range(2)]
        tb = [AT([1024], F32, "m_tb%d" % i) for i in range(2)]
        stat = [AT([24], F32, "m_stat%d" % i) for i in range(2)]
        gstl = [AT([48], F32, "m_gst%d" % i) for i in range(4)]
        gst = gstl[0]
        osbl, osql = [], []
        for i in range(2):
            o_ = TB(xin[i].ap[:, 0:512], "m_osb%d" % i)
            o_.b = xin[i].b
            q_ = TB(zc[i].ap[:, 0:512], "m_osq%d" % i)
            q_.b = zc[i].b
            osbl.append(o_)
            osql.append(q_)
        screp = TB(tb[1].ap.bitcast(BF16)[:, 0:1024].rearrange("p (a b) -> p a b", a=8), "m_screp")
        screp.b = tb[1].b
        gate_table(l, 2, which, gt, screp, tb[0])
        row_table(lng_t, lng_d[l, 0:1, :], "lng")
        row_table(lnb_t, lnb_d[l, 0:1, :], "lnb")
        wall = win_d[l, :, :].rearrange("(k p) n -> p k n", p=128)
        woall = wout_d[l, :, :].rearrange("(k p) n -> p k n", p=128)
        nrow = NTOK // rowlen

        hTb = mk_hTb(gsz, "mhT")
        w0_pre = [(load_w(wall[:, :, 0:1024], [8, 1024]), load_w(wall[:, :, 1024:2048], [8, 1024]))]
        for g in range(ng):
            if g == 0:
                p.dma(SP, hT.ap, hTG[0].ap, key="hTld", reads=[hTG[0].b], writes=[bb.b for row in hTb for bb in row])
            alltiles = list(range(gsz))

            def fm(wv, ws, col, bk):
                for k in range(8):
                    T(lambda e, k=k, wv=wv, col=col, bk=bk: e.matmul(
                        bk.ap[:, 0:NTOK], wv[:, k, col:col + 128], hT.ap[:, k, :], start=(k == 0), stop=(k == 7)),
                      [ws] + hT_reads(hTb, alltiles, k), [bk])

            def tm(wv, ws, col, n, t, bk, off):
                for k in range(8):
                    T(lambda e, k=k, wv=wv, col=col, bk=bk, t=t, n=n, off=off: e.matmul(
                        bk.ap[:, off:off + n], hT.ap[:, k, t * 128:(t + 1) * 128], wv[:, k, col:col + n],
                        start=(k == 0), stop=(k == 7)), [ws, hTb[t][k // 4]], [bk])

            MS = int(os.environ.get("MSTOP", "99"))
            if MS < 1:
                return
            (w0, s0), (w1, s1) = w0_pre[0]
            for c2 in range(2):
                bx = nb()
                fm(w0, s0, c2 * 128, bx)
                A(lambda e, c2=c2, bx=bx: e.activation(xin[c2].ap[:, 0:NTOK], bx.ap[:, 0:NTOK], AF.Copy), [bx], [xin[c2]])
                bc = nb()
                fm(w0, s0, 512 + c2 * 128, bc)
                V(lambda e, c2=c2, bc=bc: e.tensor_mul(zc[c2].ap[:, 0:NTOK], bc.ap[:, 0:NTOK], xin[c2].ap[:, 0:NTOK]), [bc, xin[c2]], [zc[c2]])
                cwv = cwt.ap[:, l, c2, :]
                z3 = zc[c2].ap[:, 0:NTOK].rearrange("p (r w) -> p r w", w=rowlen)
                y3 = yc[c2].ap[:, 0:NTOK].rearrange("p (r w) -> p r w", w=rowlen)
                V(lambda e, c2=c2, cwv=cwv: e.tensor_scalar_mul(yc[c2].ap[:, 0:NTOK], zc[c2].ap[:, 0:NTOK], cwv[:, 1:2]), [zc[c2], cwt], [yc[c2]])
                V(lambda e, c2=c2, cwv=cwv, z3=z3, y3=y3: e.scalar_tensor_tensor(
                    y3[:, :, 1:rowlen], z3[:, :, 0:rowlen - 1], cwv[:, 0:1], y3[:, :, 1:rowlen],
                    op0=ALU.mult, op1=ALU.add), [zc[c2], cwt, yc[c2]], [yc[c2]])
                V(lambda e, c2=c2, cwv=cwv, z3=z3, y3=y3: e.scalar_tensor_tensor(
                    y3[:, :, 0:rowlen - 1], z3[:, :, 1:rowlen], cwv[:, 2:3], y3[:, :, 0:rowlen - 1],
                    op0=ALU.mult, op1=ALU.add), [zc[c2], cwt, yc[c2]], [yc[c2]])
                bg = nb()
                fm(w0, s0, 256 + c2 * 128, bg)
                V(lambda e, c2=c2, bg=bg: e.tensor_mul(yT.ap[:, c2, :], bg.ap[:, 0:NTOK], yc[c2].ap[:, 0:NTOK]), [bg, yc[c2]], [yT])
            for t in range(gsz):
                bu = nb()
                tm(w0, s0, 768, 256, t, bu, 0)
                gelu(bu, ug[t], gtmp)
            if MS < 2:
                return
            for t in range(gsz):
                bv = nb()
                tm(w1, s1, 0, 256, t, bv, 0)
                vg = vgl[t % 2]
                gelu(bv, vg, gtmp)
                gs = gstl[t % 4]
                V(lambda e, vg=vg, gs=gs: e.bn_stats(gs.ap[:, 32:38], vg.ap), [vg], [gs])
                V(lambda e, gs=gs: e.bn_aggr(gs.ap[:, 38:40], gs.ap[:, 32:38]), [gs], [gs])
                A(lambda e, gs=gs: e.activation(gs.ap[:, 40:41], gs.ap[:, 39:40], AF.Sqrt, bias=epsc.ap[:, 0:1]), [gs, epsc], [gs])
                V(lambda e, gs=gs: e.reciprocal(gs.ap[:, 41:42], gs.ap[:, 40:41]), [gs], [gs])
                V(lambda e, vg=vg, gs=gs: e.tensor_scalar(vg.ap, vg.ap, gs.ap[:, 38:39], gs.ap[:, 41:42],
                                                          op0=ALU.subtract, op1=ALU.mult), [vg, gs], [vg])
                V(lambda e, vg=vg: e.tensor_mul(vg.ap, vg.ap, glng.ap), [vg, glng], [vg])
                V(lambda e, t=t, vg=vg: e.tensor_add(vln[t].ap, vg.ap, glnb.ap), [vg, glnb], [vln[t]])
            for c4 in range(4):
                bq = nb()
                fm(w1, s1, 256 + c4 * 128, bq)
                A(lambda e, c4=c4, bq=bq: e.activation(qT.ap[:, c4, :], bq.ap[:, 0:NTOK], AF.Copy, scale=0.125), [bq], [qT])
                V(lambda e, c4=c4, bq=bq: e.tensor_tensor(
                    qdfT.ap[:, c4, :].rearrange("p (t i) -> p t i", i=128), bq.ap[:, 0:NTOK].rearrange("p (t i) -> p t i", i=128),
                    dqf.ap[:, c4, :].unsqueeze(1).to_broadcast([128, gsz, 128]), op=ALU.mult), [bq, dqf], [qdfT])
                V(lambda e, c4=c4, bq=bq: e.tensor_tensor(
                    qdbT.ap[:, c4, :].rearrange("p (t i) -> p t i", i=128), bq.ap[:, 0:NTOK].rearrange("p (t i) -> p t i", i=128),
                    dqb.ap[:, c4, :].unsqueeze(1).to_broadcast([128, gsz, 128]), op=ALU.mult), [bq, dqb], [qdbT])
            for c4 in range(2):
                bkk = nb()
                fm(w1, s1, 768 + c4 * 128, bkk)
                A(lambda e, c4=c4, bkk=bkk: e.activation(kT.ap[:, c4, :], bkk.ap[:, 0:NTOK], AF.Copy), [bkk], [kT])
            if MS < 3:
                return
            w2, s2 = load_w(wall[:, :, 2048:3072], [8, 1024])
            for c4 in range(2, 4):
                bkk = nb()
                fm(w2, s2, (c4 - 2) * 128, bkk)
                A(lambda e, c4=c4, bkk=bkk: e.activation(kT.ap[:, c4, :], bkk.ap[:, 0:NTOK], AF.Copy), [bkk], [kT])
            for t in range(gsz):
                bvr = nb()
                tm(w2, s2, 256, 512, t, bvr, 0)
                V(lambda e, t=t, bvr=bvr: e.tensor_copy(vrb[t].ap, bvr.ap), [bvr], [vrb[t]])
            w3, s3 = load_w(wall[:, :, 3072:3328], [8, 256])
            for t in range(gsz):
                bgt = nb()
                tm(w2, s2, 768, 256, t, bgt, 0)
                tm(w3, s3, 0, 256, t, bgt, 256)
                A(lambda e, t=t, bgt=bgt: e.activation(sg[t].ap, bgt.ap, AF.Silu), [bgt], [sg[t]])
            if MS < 4:
                return
            wo, so = load_w(woall[:, :, :], [8, 1024])
            if g + 1 < ng:
                w0_pre[0] = (load_w(wall[:, :, 0:1024], [8, 1024]), load_w(wall[:, :, 1024:2048], [8, 1024]))
            if g + 1 < ng:
                p.dma(SP, hT.ap, hTG[g + 1].ap, key="hTld", reads=[hTG[g + 1].b],
                      writes=[bb.b for row in hTb for bb in row])
            xrm = [xr[0], xr[1], xl[0], xl[1]]
            for t in range(gsz):
                ld(xrm[t], tiles_src[g * gsz + t].ap, "mxr%d" % t, src_tb=tiles_src[g * gsz + t])
            st = [dict() for _ in range(gsz)]

            def stA(t):
                c = g * gsz + t
                tsl = slice(t * 128, (t + 1) * 128)
                d = st[t]
                bm = nb()
                for h in range(4):
                    T(lambda e, h=h, t=t, bm=bm: e.matmul(bm.ap[:, h * 64:(h + 1) * 64], wsTb.ap[:, h, :],
                                                           vln[t].ap[:, h * 64:(h + 1) * 64], start=True, stop=True),
                      [wsTb, vln[t]], [bm])
                bsA, bsB = nb(), nb()
                for h in range(8):
                    pr, hl = h // 2, h % 2
                    ps_ = slice(hl * 64, hl * 64 + 64)
                    bs_ = bsA if hl == 0 else bsB
                    T(lambda e, pr=pr, ps_=ps_, tsl=tsl, bs_=bs_: e.matmul(
                        bs_.ap[:, pr * 128:(pr + 1) * 128], kT.ap[ps_, pr, tsl], qT.ap[ps_, pr, tsl],
                        start=True, stop=True), [kT, qT], [bs_])
                yg = ygm[t % 2]
                for h in range(4):
                    V(lambda e, h=h, t=t, bm=bm, yg=yg: e.scalar_tensor_tensor(
                        yg.ap[:, h * 64:(h + 1) * 64], bm.ap[:, h * 64:(h + 1) * 64], bsT.ap[:, l, h:h + 1],
                        ug[t].ap[:, h * 64:(h + 1) * 64], op0=ALU.add, op1=ALU.mult), [bm, bsT, ug[t]], [yg])
                for hl, bs_ in enumerate((bsA, bsB)):
                    pt = PT[t % 2][hl]
                    V(lambda e, hl=hl, bs_=bs_, pt=pt: e.tensor_tensor(
                        pt.ap[:, 0:4, :], bs_.ap.rearrange("p (a b) -> p a b", a=4), DT.ap[:, hl::2, :],
                        op=ALU.mult), [bs_, DT], [pt])

            def stB(t):
                c = g * gsz + t
                tsl = slice(t * 128, (t + 1) * 128)
                yg = ygm[t % 2]
                btr = nb()
                btv = btr.ap
                for j in range(2):
                    T(lambda e, j=j, btv=btv, yg=yg: e.transpose(btv[:, j * 128:(j + 1) * 128], yg.ap[:, j * 128:(j + 1) * 128],
                                                                  ident.ap), [yg, ident], [btr])
                bo = nb()
                st[t]["bo"] = bo
                for h in range(8):
                    pr, hl = h // 2, h % 2
                    ps_ = slice(hl * 64, hl * 64 + 64)
                    osl = slice(h * 64, (h + 1) * 64)
                    pt = PT[t % 2][hl]
                    T(lambda e, pr=pr, t=t, osl=osl, pt=pt, bo=bo: e.matmul(
                        bo.ap[:, osl], pt.ap[:, pr, :], vrb[t].ap[:, osl], start=True, stop=False), [pt, vrb[t]], [bo])
                    T(lambda e, pr=pr, ps_=ps_, tsl=tsl, osl=osl, c=c, bo=bo: e.matmul(
                        bo.ap[:, osl], qdfT.ap[ps_, pr, tsl], SFo.ap[ps_, c, pr, :], start=False, stop=False),
                      [qdfT, SFo], [bo])
                    T(lambda e, pr=pr, ps_=ps_, tsl=tsl, osl=osl, c=c, bo=bo: e.matmul(
                        bo.ap[:, osl], qdbT.ap[ps_, pr, tsl], SBo.ap[ps_, c, pr, :], start=False, stop=True),
                      [qdbT, SBo], [bo])
                A(lambda e, tsl=tsl, btv=btv: e.activation(
                    yT.ap[:, 2:4, tsl], btv[:, 0:256].rearrange("p (a b) -> p a b", a=2), AF.Copy), [btr], [yT])
                osb, osq = osbl[t % 2], osql[t % 2]
                A(lambda e, bo=bo, osb=osb: e.activation(osb.ap, bo.ap, AF.Copy), [bo], [osb])
                A(lambda e, bo=bo, osq=osq: e.activation(osq.ap, bo.ap, AF.Square), [bo], [osq])

            def stC(t):
                osb, osq = osbl[t % 2], osql[t % 2]
                gs = gstl[t % 4]
                o3 = osb.ap.rearrange("p (h e) -> p h e", h=8)
                q3 = osq.ap.rearrange("p (h e) -> p h e", h=8)
                V(lambda e: e.tensor_reduce(gs.ap[:, 0:8], o3, axis=AX.X, op=ALU.add), [osb], [gs])
                V(lambda e: e.tensor_reduce(gs.ap[:, 8:16], q3, axis=AX.X, op=ALU.add), [osq], [gs])
                V(lambda e: e.tensor_scalar_mul(gs.ap[:, 0:8], gs.ap[:, 0:8], 1.0 / 64), [gs], [gs])
                V(lambda e: e.tensor_mul(gs.ap[:, 16:24], gs.ap[:, 0:8], gs.ap[:, 0:8]), [gs], [gs])
                V(lambda e: e.scalar_tensor_tensor(gs.ap[:, 8:16], gs.ap[:, 8:16], 1.0 / 64, gs.ap[:, 16:24],
                                                   op0=ALU.mult, op1=ALU.subtract), [gs], [gs])
                A(lambda e: e.activation(gs.ap[:, 16:24], gs.ap[:, 8:16], AF.Sqrt, bias=epsc.ap[:, 0:1]), [gs, epsc], [gs])
                V(lambda e: e.reciprocal(gs.ap[:, 24:32], gs.ap[:, 16:24]), [gs], [gs])
                V(lambda e: e.tensor_tensor(o3, o3, gs.ap[:, 0:8].unsqueeze(2).to_broadcast([128, 8, 64]),
                                            op=ALU.subtract), [osb, gs], [osb])
                V(lambda e: e.tensor_tensor(o3, o3, gs.ap[:, 24:32].unsqueeze(2).to_broadcast([128, 8, 64]),
                                            op=ALU.mult), [osb, gs], [osb])
                V(lambda e: e.tensor_mul(osq.ap, osb.ap, sg[t].ap), [osb, sg[t]], [osq])

            def stD(t):
                tsl = slice(t * 128, (t + 1) * 128)
                yrt = osql[t % 2]
                btr2 = nb()
                btv2 = btr2.ap
                for j in range(4):
                    T(lambda e, j=j, btv2=btv2: e.transpose(btv2[:, j * 128:(j + 1) * 128], yrt.ap[:, j * 128:(j + 1) * 128],
                                                             ident.ap), [yrt, ident], [btr2])
                A(lambda e, tsl=tsl, btv2=btv2: e.activation(
                    yT.ap[:, 4:8, tsl], btv2[:, 0:512].rearrange("p (a b) -> p a b", a=4), AF.Copy), [btr2], [yT])

            def stE(t):
                tsl = slice(t * 128, (t + 1) * 128)
                bA, bB = nb(), nb()
                st[t]["bAB"] = (bA, bB)
                for hf, bk in enumerate((bA, bB)):
                    for k in range(8):
                        T(lambda e, k=k, hf=hf, bk=bk, tsl=tsl, wo=wo: e.matmul(
                            bk.ap, yT.ap[:, k, tsl], wo[:, k, hf * 512:(hf + 1) * 512], start=(k == 0), stop=(k == 7)),
                          [yT, so], [bk])

            def stF(t):
                c = g * gsz + t
                bA, bB = st[t]["bAB"]
                layer_norm_store(l, 0, bA, bB, xrm[t], gt, lng_t, lnb_t, tb[t % 2], stat[t % 2],
                                 tiles_dst[c], "st%d" % (t % 2), store_eng=SP)

            stages = [stA, stB, stC, stD, stE, stF]
            for step in range(gsz + len(stages) - 1):
                for si in range(len(stages) - 1, -1, -1):
                    t = step - si
                    if 0 <= t < gsz:
                        stages[si](t)

    def ffn(l, tiles_src, tiles_dst, nt_total, which):
        phase_switch()
        if stop[0]:
            return
        gsz = min(NTF, nt_total)
        ng = nt_total // gsz
        NTOK = gsz * 128
        gt = AT([1024], F32, "f_gt")
        lng_t = AT([1024], F32, "f_lng")
        lnb_t = AT([1024], F32, "f_lnb")
        hTs = [AT([8, NTOK], BF16, "f_hT%d" % i) for i in range(2)]
        aT = AT([32, NTOK], BF16, "f_aT")
        t1 = [AT([1024], F32, "f_t1_%d" % i) for i in range(gsz)]
        rl = [AT([NTOK], F32, "f_rl%d" % i) for i in range(2)]
        stat = [AT([24], F32, "f_stat%d" % i) for i in range(2)]
        xr4 = [xr[0], xr[1]] + [AT([1024], F32, "f_xr%d" % i) for i in range(2, gsz)]
        screp = TB(t1[1].ap.bitcast(BF16)[:, 0:1024].rearrange("p (a b) -> p a b", a=8), "f_screp")
        screp.b = t1[1].b
        gate_table(l, 5, which, gt, screp, t1[0])
        row_table(lng_t, lng_d[l, 1:2, :], "lng")
        row_table(lnb_t, lnb_d[l, 1:2, :], "lnb")
        w1all = wff1_d[l, :, :].rearrange("(k p) n -> p k n", p=128)
        w2all = wff2_d[l, :, :].rearrange("(k p) n -> p k n", p=128)
        hTbs = [mk_hTb(gsz, "fhT%d" % i) for i in range(2)]
        alltiles = list(range(gsz))

        def grp(g):
            return tiles_src[g * gsz:(g + 1) * gsz]

        make_hT(grp(0), gsz, which, 3, 4, hTs[0], hTbs[0], next_tiles=grp(1) if ng > 1 else None)
        for g in range(ng):
            hT, hTb = hTs[g % 2], hTbs[g % 2]
            for pc in range(4):
                wv, ws = load_w(w1all[:, :, pc * 1024:(pc + 1) * 1024], [8, 1024])
                for j in range(8):
                    fc = pc * 8 + j
                    bk = nb()
                    for k in range(8):
                        T(lambda e, k=k, j=j, wv=wv, bk=bk, hT=hT: e.matmul(
                            bk.ap[:, 0:NTOK], wv[:, k, j * 128:(j + 1) * 128], hT.ap[:, k, :], start=(k == 0), stop=(k == 7)),
                          [ws] + hT_reads(hTb, alltiles, k), [bk])
                    r = rl[fc % 2]
                    A(lambda e, bk=bk, r=r: e.activation(r.ap, bk.ap[:, 0:NTOK], AF.Relu), [bk], [r])
                    A(lambda e, fc=fc, r=r: e.activation(aT.ap[:, fc, :], r.ap, AF.Square), [r], [aT])
                if g + 1 < ng and pc < gsz:
                    make_hT(grp(g + 1), gsz, which, 3, 4, hTs[(g + 1) % 2], hTbs[(g + 1) % 2], only=[pc],
                            next_tiles=(grp(g + 2) if (pc == gsz - 1 and g + 2 < ng) else None))
            for t in range(gsz):
                c = g * gsz + t
                ld(xr4[t], tiles_src[c].ap, "fxr%d" % t, src_tb=tiles_src[c])
            for q in range(4):
                wv, ws = load_w(w2all[:, :, q * 256:(q + 1) * 256], [32, 256])
                for t in range(gsz):
                    bk = nb()
                    for k in range(32):
                        T(lambda e, k=k, t=t, wv=wv, bk=bk: e.matmul(
                            bk.ap[:, 0:256], aT.ap[:, k, t * 128:(t + 1) * 128], wv[:, k, :], start=(k == 0), stop=(k == 31)),
                          [ws, aT], [bk])
                    V(lambda e, q=q, t=t, bk=bk: e.tensor_mul(t1[t].ap[:, q * 256:(q + 1) * 256], bk.ap[:, 0:256],
                                                               gt.ap[:, q * 256:(q + 1) * 256]), [bk, gt], [t1[t]])
                    if q == 3:
                        c = g * gsz + t
                        ln_tail(xr4[t], t1[t], lng_t, lnb_t, stat[t % 2], tiles_dst[c], "fst%d" % t, aff=V, store_eng=SP)

    lat_src = xT
    ctx_src = ctxT
    last_out = None
    for l in range(2):
        last = (l == 1)
        layer_tables(l)
        if not stop[0]:
            modulation(l)
        cf, cb = pass1(l, ctx_src, 2, 1, zst, zst, SFc, SBc, True, hTcG)
        V(lambda e, cf=cf: e.tensor_copy(cinf.ap, cf.ap), [cf], [cinf])
        V(lambda e, cb=cb: e.tensor_copy(cinb.ap, cb.ap), [cb], [cinb])
        if not last:
            mixer(l, ctx_src, xcmT, 2, 1, SFc, SBc, 256, hTcG)
            ffn(l, xcmT, xc1T, 2, 1)
        pass1(l, lat_src, 32, 0, cinf, cinb, SF, SB, True, hTlG)
        mixer(l, lat_src, x1T, 32, 0, SF, SB, 64, hTlG)
        dst = yT_ if last else x2T
        ffn(l, x1T, dst, 32, 0)
        lat_src = x2T
        ctx_src = xc1T
    finals = [dst_t.b.w for dst_t in yT_ if dst_t.b.w is not None]
    if stop[0]:
        for lst in (x1T, x2T, xcmT, xc1T):
            finals += [t.b.w for t in lst if t.b.w is not None and t.b.w.is_dma]
    p.finalize(final_waits=finals)
    return nc


_CACHE = {}


def _prep_inputs(inp, b):
    f = np.float32
    c = np.asarray(inp["c"], f)[b]
    cctx = np.asarray(inp["c_ctx"], f)
    cc = np.stack([c.reshape(8, 128).T, cctx.reshape(8, 128).T], axis=-1)
    b_ada = np.asarray(inp["b_ada"], f)
    b_col = np.ascontiguousarray(b_ada.reshape(2, 48, 128).transpose(2, 0, 1))
    conv_w = np.asarray(inp["conv_w"], f)
    cw = np.ascontiguousarray(conv_w.reshape(2, 3, 2, 128).transpose(3, 0, 2, 1))
    ws = np.asarray(inp["gmlp_ws"], f)
    wsT = np.ascontiguousarray(ws.transpose(0, 3, 1, 2))
    bs = np.asarray(inp["gmlp_bs"], f)
    bsT = np.ascontiguousarray(bs.transpose(2, 0, 1))
    dec = np.concatenate([np.asarray(inp["ret_decay_fwd"], f).reshape(-1),
                          np.asarray(inp["ret_decay_bwd"], f).reshape(-1)])
    dec = np.ascontiguousarray(np.broadcast_to(dec[None, :], (128, 32)))
    return {
        "x": np.ascontiguousarray(np.asarray(inp["x"], f)[b]),
        "ctx": np.ascontiguousarray(np.asarray(inp["ctx"], f)[b]),
        "cc": np.ascontiguousarray(cc),
        "w_ada": np.asarray(inp["w_ada"], f),
        "b_col": b_col,
        "b_ada": b_ada,
        "w_in": np.asarray(inp["w_in"], f),
        "cw": cw,
        "gln_g": np.asarray(inp["gmlp_ln_g"], f),
        "gln_b": np.asarray(inp["gmlp_ln_b"], f),
        "wsT": wsT,
        "bsT": bsT,
        "dec": dec,
        "w_out": np.asarray(inp["w_out"], f),
        "w_ff1": np.asarray(inp["w_ff1"], f),
        "w_ff2": np.asarray(inp["w_ff2"], f),
        "ln_g": np.asarray(inp["ln_g"], f),
        "ln_b": np.asarray(inp["ln_b"], f),
    }


def kernel(**inputs):
    if "nc" not in _CACHE:
        _CACHE["nc"] = build_program()
    nc = _CACHE["nc"]
    in_maps = [_prep_inputs(inputs, b) for b in range(8)]
    res = run_bass_kernel_spmd(nc, in_maps, core_ids=list(range(8)))
    return np.stack([np.asarray(r["y"], np.float32) for r in res.results], axis=0)
```
